# Optimizing a Trainium2 kernel written in Bass

```python
import jax
import jax.numpy as jnp
from jax import lax
import numpy as np

D_MODEL = 1024
BATCH = 1
SEQ = 16384
DEPTH = 2

GRID_W = 64
CTX_LEN = 256
CHUNK = 128
EPS = 1e-6
ROPE_THETA = 10000.0
N_MOD = 9

A_WIDTH = D_MODEL // 2
A_GROUPS = 8
A_GROUP_DIM = A_WIDTH // A_GROUPS
ATT_HEADS = 8
ATT_KV_HEADS = 2
ATT_GROUP = ATT_HEADS // ATT_KV_HEADS
ATT_HEAD_DIM = (D_MODEL // 2) // ATT_HEADS
ATT_Q = ATT_HEADS * ATT_HEAD_DIM
ATT_KV = ATT_KV_HEADS * ATT_HEAD_DIM
EVEN_IN = 2 * A_WIDTH + ATT_Q + 2 * ATT_KV
EVEN_CAT = A_WIDTH + ATT_Q

ML_HEADS = 8
ML_QK_DIM = (D_MODEL // 2) // ML_HEADS
ML_V_DIM = D_MODEL // ML_HEADS
ML_QK = ML_HEADS * ML_QK_DIM
ML_V = ML_HEADS * ML_V_DIM
CONV_WIDTH = 3
ODD_IN = 2 * ML_QK + 2 * ML_V + 4 * ML_HEADS

FFN_HIDDEN = ((8 * D_MODEL // 3 + 127) // 128) * 128
N_EVEN = (DEPTH + 1) // 2
N_ODD = DEPTH // 2

kernel_name = "hybrid_gmlp_gqa_mlstm_dit_trunk"


def rmsnorm(x, g):
    xf = x.astype(jnp.float32)
    y = xf * lax.rsqrt(jnp.mean(xf * xf, axis=-1, keepdims=True) + EPS)
    return (y * g.astype(jnp.float32)).astype(x.dtype)


def modulate(z, g, shift, scale):
    return rmsnorm(z, g) * (1 + scale) + shift


def swiglu(z, w13, w2):
    gate, up = jnp.split(z @ w13, 2, axis=-1)
    return (jax.nn.silu(gate) * up) @ w2


def axial_rope_angles(n_tokens):
    rows = n_tokens // GRID_W
    axis_dim = ATT_HEAD_DIM // 2
    inv_freq = ROPE_THETA ** (-jnp.arange(0, axis_dim, 2, dtype=jnp.float32) / axis_dim)
    row = jnp.broadcast_to(jnp.arange(rows, dtype=jnp.float32)[:, None], (rows, GRID_W)).reshape(-1)
    col = jnp.broadcast_to(jnp.arange(GRID_W, dtype=jnp.float32)[None, :], (rows, GRID_W)).reshape(-1)
    return row[:, None] * inv_freq, col[:, None] * inv_freq


def rotate(x, ang):
    x1, x2 = jnp.split(x, 2, axis=-1)
    cos = jnp.cos(ang)[None, :, None, :].astype(x.dtype)
    sin = jnp.sin(ang)[None, :, None, :].astype(x.dtype)
    return jnp.concatenate([x1 * cos - x2 * sin, x2 * cos + x1 * sin], axis=-1)


def axial_rope(x, ang_row, ang_col):
    xr, xc = jnp.split(x, 2, axis=-1)
    return jnp.concatenate([rotate(xr, ang_row), rotate(xc, ang_col)], axis=-1)


def chunk_spatial_gate(u, v, norm_g, w_s, b_s):
    b, t, _ = v.shape
    vn = rmsnorm(v.reshape(b, t, A_GROUPS, A_GROUP_DIM), norm_g.reshape(A_GROUPS, A_GROUP_DIM))
    vn = vn.reshape(b, t // CHUNK, CHUNK, A_GROUPS, A_GROUP_DIM)
    s = jnp.einsum('gts,bnsgd->bntgd', w_s, vn) + b_s.T[None, None, :, :, None]
    return u * s.reshape(b, t, A_WIDTH)


def gqa_attend(q, k, v):
    b, nq = q.shape[0], q.shape[1]
    qg = q.reshape(b, nq, ATT_KV_HEADS, ATT_GROUP, ATT_HEAD_DIM)
    s = jnp.einsum('bqkgd,bskd->bkgqs', qg, k).astype(jnp.float32) * (ATT_HEAD_DIM ** -0.5)
    p = jax.nn.softmax(s, axis=-1).astype(v.dtype)
    return jnp.einsum('bkgqs,bskd->bqkgd', p, v).reshape(b, nq, ATT_Q)


def gqa_blocks(q, k, v):
    b, t = q.shape[0], q.shape[1]
    qb = jnp.swapaxes(q.reshape(b, t // CHUNK, CHUNK, ATT_HEADS, ATT_HEAD_DIM), 0, 1)
    out = lax.map(lambda qq: gqa_attend(qq, k, v), qb)
    return jnp.swapaxes(out, 0, 1).reshape(b, t, ATT_Q)


def gmlp_gqa_mixer(z_lat, z_ctx, w_in, gate_norm_g, sp_w, sp_b, q_norm_g, k_norm_g, w_out,
                   ang_row, ang_col, ctx_out):
    def project(z):
        p = z @ w_in
        b, t, _ = p.shape
        u = jax.nn.gelu(p[..., :A_WIDTH])
        v = jax.nn.gelu(p[..., A_WIDTH:2 * A_WIDTH])
        o = 2 * A_WIDTH
        q = rmsnorm(p[..., o:o + ATT_Q].reshape(b, t, ATT_HEADS, ATT_HEAD_DIM), q_norm_g)
        k = rmsnorm(p[..., o + ATT_Q:o + ATT_Q + ATT_KV].reshape(b, t, ATT_KV_HEADS, ATT_HEAD_DIM), k_norm_g)
        va = p[..., o + ATT_Q + ATT_KV:].reshape(b, t, ATT_KV_HEADS, ATT_HEAD_DIM)
        return u, v, q, k, va

    u_l, v_l, q_l, k_l, a_l = project(z_lat)
    u_c, v_c, q_c, k_c, a_c = project(z_ctx)
    q_l = axial_rope(q_l, ang_row, ang_col)
    k_l = axial_rope(k_l, ang_row, ang_col)
    k_all = jnp.concatenate([k_c, k_l], axis=1)
    a_all = jnp.concatenate([a_c, a_l], axis=1)
    y_lat = jnp.concatenate([chunk_spatial_gate(u_l, v_l, gate_norm_g, sp_w, sp_b),
                             gqa_blocks(q_l, k_all, a_all)], axis=-1) @ w_out
    if not ctx_out:
        return y_lat, None
    y_ctx = jnp.concatenate([chunk_spatial_gate(u_c, v_c, gate_norm_g, sp_w, sp_b),
                             gqa_attend(q_c, k_c, a_c)], axis=-1) @ w_out
    return y_lat, y_ctx


def short_conv(x, w):
    pad = CONV_WIDTH // 2
    t = x.shape[1]
    xp = jnp.pad(x, ((0, 0), (pad, pad), (0, 0)))
    y = xp[:, 0:t] * w[0]
    for j in range(1, CONV_WIDTH):
        y = y + xp[:, j:j + t] * w[j]
    return y


def to_chunks(t):
    b, n = t.shape[0], t.shape[1] // CHUNK
    t = t.reshape((b, n, CHUNK) + t.shape[2:])
    return jnp.moveaxis(t, 3, 1)


def from_chunks(t):
    t = jnp.moveaxis(t, 1, 3)
    return t.reshape((t.shape[0], t.shape[1] * t.shape[2]) + t.shape[3:])


def mlstm_chunk_states(k, v, log_i, log_f, state0):
    b_cum = jnp.cumsum(log_f, axis=-1)
    b_last = b_cum[..., -1]
    w = b_last[..., None] - b_cum + log_i
    m_loc = jnp.max(w, axis=-1)
    e = jnp.exp(w - m_loc[..., None])
    c_loc = jnp.einsum('bhnl,bhnlv,bhnlk->bhnvk', e, v, k)
    n_loc = jnp.einsum('bhnl,bhnlk->bhnk', e, k)

    def step(carry, inp):
        c_prev, n_prev, m_prev = carry
        cl, nl, ml, bl = inp
        m_new = jnp.maximum(bl + m_prev, ml)
        a = jnp.exp(bl + m_prev - m_new)
        g = jnp.exp(ml - m_new)
        c_new = a[..., None, None] * c_prev + g[..., None, None] * cl
        n_new = a[..., None] * n_prev + g[..., None] * nl
        return (c_new, n_new, m_new), (c_prev, n_prev, m_prev)

    xs = (jnp.moveaxis(c_loc, 2, 0), jnp.moveaxis(n_loc, 2, 0),
          jnp.moveaxis(m_loc, 2, 0), jnp.moveaxis(b_last, 2, 0))
    final, (c_in, n_in, m_in) = lax.scan(step, state0, xs)
    return jnp.moveaxis(c_in, 0, 2), jnp.moveaxis(n_in, 0, 2), jnp.moveaxis(m_in, 0, 2), final


def mlstm_chunk_outputs(q, k, v, log_i, log_f, c_in, n_in, m_in):
    b_cum = jnp.cumsum(log_f, axis=-1)
    ln = q.shape[-2]
    lower = jnp.tril(jnp.ones((ln, ln), dtype=bool))
    dmat = b_cum[..., :, None] - b_cum[..., None, :] + log_i[..., None, :]
    dmat = jnp.where(lower, dmat, -jnp.inf)
    inter = b_cum + m_in[..., None]
    m = jnp.maximum(inter, jnp.max(dmat, axis=-1))
    wts = jnp.exp(dmat - m[..., None])
    a = jnp.exp(inter - m)
    s = jnp.einsum('bhnjd,bhnsd->bhnjs', q, k) * wts
    num = jnp.einsum('bhnjs,bhnsv->bhnjv', s, v) + a[..., None] * jnp.einsum('bhnvk,bhnjk->bhnjv', c_in, q)
    den = jnp.sum(s, axis=-1) + a * jnp.einsum('bhnk,bhnjk->bhnj', n_in, q)
    return num / jnp.maximum(jnp.abs(den), jnp.exp(-m))[..., None]


def mlstm_mixer(z_lat, z_ctx, w_in, conv_w, gate_b, out_norm_g, w_out, ctx_out):
    def project(z):
        p = z @ w_in
        b, t, _ = p.shape
        qk = jax.nn.silu(short_conv(p[..., :2 * ML_QK], conv_w)).astype(jnp.float32)
        q = qk[..., :ML_QK].reshape(b, t, ML_HEADS, ML_QK_DIM) * (ML_QK_DIM ** -0.5)
        k = qk[..., ML_QK:].reshape(b, t, ML_HEADS, ML_QK_DIM)
        v = p[..., 2 * ML_QK:2 * ML_QK + ML_V].reshape(b, t, ML_HEADS, ML_V_DIM).astype(jnp.float32)
        o = p[..., 2 * ML_QK + ML_V:2 * ML_QK + 2 * ML_V]
        g = (p[..., 2 * ML_QK + 2 * ML_V:] + gate_b).astype(jnp.float32).reshape(b, t, 4, ML_HEADS)
        return q, k, v, o, g

    lat = project(z_lat)
    ctx = project(z_ctx)
    bsz = z_lat.shape[0]
    zero = (jnp.zeros((bsz, ML_HEADS, ML_V_DIM, ML_QK_DIM), jnp.float32),
            jnp.zeros((bsz, ML_HEADS, ML_QK_DIM), jnp.float32),
            jnp.zeros((bsz, ML_HEADS), jnp.float32))

    def run_direction(d):
        def orient(t):
            return jnp.flip(t, axis=1) if d == 1 else t

        def chunked(side):
            q, k, v, _, g = side
            log_i = g[:, :, 2 * d]
            log_f = jax.nn.log_sigmoid(g[:, :, 2 * d + 1])
            return tuple(to_chunks(orient(t)) for t in (q, k, v, log_i, log_f))

        qc, kc, vc, ic, fc = chunked(ctx)
        ql, kl, vl, il, fl = chunked(lat)
        cc_in, nc_in, mc_in, ctx_final = mlstm_chunk_states(kc, vc, ic, fc, zero)
        cl_in, nl_in, ml_in, _ = mlstm_chunk_states(kl, vl, il, fl, ctx_final)
        h_lat = orient(from_chunks(mlstm_chunk_outputs(ql, kl, vl, il, fl, cl_in, nl_in, ml_in)))
        h_ctx = orient(from_chunks(mlstm_chunk_outputs(qc, kc, vc, ic, fc, cc_in, nc_in, mc_in))) if ctx_out else None
        return h_lat, h_ctx

    hf_lat, hf_ctx = run_direction(0)
    hb_lat, hb_ctx = run_direction(1)

    def readout(h, o):
        b, t = h.shape[0], h.shape[1]
        hn = rmsnorm(h, out_norm_g.reshape(ML_HEADS, ML_V_DIM)).reshape(b, t, ML_V).astype(o.dtype)
        return (hn * jax.nn.sigmoid(o)) @ w_out

    y_lat = readout(hf_lat + hb_lat, lat[3])
    y_ctx = readout(hf_ctx + hb_ctx, ctx[3]) if ctx_out else None
    return y_lat, y_ctx


def setup_inputs(seed: int = 0) -> dict:
    key = jax.random.key(seed)
    ks = jax.random.split(key, 24)

    def nrm(k, shape, scale=1.0):
        return jax.random.normal(k, shape, jnp.float32) * scale

    d, f = D_MODEL, FFN_HIDDEN
    gi = nrm(ks[18], (N_ODD, 2, ML_HEADS), 0.1)
    gf = jnp.linspace(3.0, 6.0, ML_HEADS, dtype=jnp.float32) + nrm(ks[19], (N_ODD, 2, ML_HEADS), 0.1)
    ml_gate_b = jnp.stack([gi[:, 0], gf[:, 0], gi[:, 1], gf[:, 1]], axis=1).reshape(N_ODD, 4 * ML_HEADS)
    return {
        "x": nrm(ks[0], (BATCH, SEQ, d)),
        "c": nrm(ks[1], (BATCH, d)),
        "ctx": nrm(ks[2], (BATCH, CTX_LEN, d)),
        "c_ctx": nrm(ks[3], (d,)),
        "mod_w": nrm(ks[4], (DEPTH, d, N_MOD * d), 0.5 * d ** -0.5),
        "mod_b": nrm(ks[5], (DEPTH, N_MOD * d), 0.1),
        "norm_g": 1.0 + nrm(ks[6], (DEPTH, 3, d), 0.1),
        "ffn_w13": nrm(ks[7], (DEPTH, 2, d, 2 * f), d ** -0.5),
        "ffn_w2": nrm(ks[8], (DEPTH, 2, f, d), f ** -0.5),
        "ab_w_in": nrm(ks[9], (N_EVEN, d, EVEN_IN), d ** -0.5),
        "ab_gate_norm_g": 1.0 + nrm(ks[10], (N_EVEN, A_WIDTH), 0.1),
        "ab_spatial_w": nrm(ks[11], (N_EVEN, A_GROUPS, CHUNK, CHUNK), CHUNK ** -0.5),
        "ab_spatial_b": 1.0 + nrm(ks[12], (N_EVEN, A_GROUPS, CHUNK), 0.1),
        "ab_q_norm_g": 1.0 + nrm(ks[13], (N_EVEN, ATT_HEAD_DIM), 0.1),
        "ab_k_norm_g": 1.0 + nrm(ks[14], (N_EVEN, ATT_HEAD_DIM), 0.1),
        "ab_w_out": nrm(ks[15], (N_EVEN, EVEN_CAT, d), EVEN_CAT ** -0.5),
        "ml_w_in": nrm(ks[16], (N_ODD, d, ODD_IN), d ** -0.5),
        "ml_conv_w": nrm(ks[17], (N_ODD, CONV_WIDTH, 2 * ML_QK), CONV_WIDTH ** -0.5),
        "ml_gate_b": ml_gate_b,
        "ml_out_norm_g": 1.0 + nrm(ks[20], (N_ODD, ML_V), 0.1),
        "ml_w_out": nrm(ks[21], (N_ODD, ML_V, d), ML_V ** -0.5),
        "final_norm_g": 1.0 + nrm(ks[22], (d,), 0.1),
    }


def reference(x, c, ctx, c_ctx, mod_w, mod_b, norm_g, ffn_w13, ffn_w2,
              ab_w_in, ab_gate_norm_g, ab_spatial_w, ab_spatial_b, ab_q_norm_g, ab_k_norm_g, ab_w_out,
              ml_w_in, ml_conv_w, ml_gate_b, ml_out_norm_g, ml_w_out, final_norm_g):
    b, t, d = x.shape
    ang_row, ang_col = axial_rope_angles(t)
    s_lat = jax.nn.silu(c)
    s_ctx = jax.nn.silu(c_ctx)[None, :]
    h, hc = x, ctx
    for l in range(DEPTH):
        keep_ctx = l < DEPTH - 1
        m_l = (s_lat @ mod_w[l] + mod_b[l]).reshape(b, N_MOD, 1, d)
        m_c = (s_ctx @ mod_w[l] + mod_b[l]).reshape(1, N_MOD, 1, d)
        h = h + 0.5 * m_l[:, 2] * swiglu(modulate(h, norm_g[l, 0], m_l[:, 0], m_l[:, 1]), ffn_w13[l, 0], ffn_w2[l, 0])
        hc = hc + 0.5 * m_c[:, 2] * swiglu(modulate(hc, norm_g[l, 0], m_c[:, 0], m_c[:, 1]), ffn_w13[l, 0], ffn_w2[l, 0])
        z_l = modulate(h, norm_g[l, 1], m_l[:, 3], m_l[:, 4])
        z_c = modulate(hc, norm_g[l, 1], m_c[:, 3], m_c[:, 4])
        if l % 2 == 0:
            e = l // 2
            y_l, y_c = gmlp_gqa_mixer(z_l, z_c, ab_w_in[e], ab_gate_norm_g[e], ab_spatial_w[e], ab_spatial_b[e],
                                      ab_q_norm_g[e], ab_k_norm_g[e], ab_w_out[e], ang_row, ang_col, keep_ctx)
        else:
            o = l // 2
            y_l, y_c = mlstm_mixer(z_l, z_c, ml_w_in[o], ml_conv_w[o], ml_gate_b[o], ml_out_norm_g[o],
                                   ml_w_out[o], keep_ctx)
        h = h + m_l[:, 5] * y_l
        h = h + 0.5 * m_l[:, 8] * swiglu(modulate(h, norm_g[l, 2], m_l[:, 6], m_l[:, 7]), ffn_w13[l, 1], ffn_w2[l, 1])
        if keep_ctx:
            hc = hc + m_c[:, 5] * y_c
            hc = hc + 0.5 * m_c[:, 8] * swiglu(modulate(hc, norm_g[l, 2], m_c[:, 6], m_c[:, 7]), ffn_w13[l, 1], ffn_w2[l, 1])
    return rmsnorm(h, final_norm_g)
```

```python
import numpy as np
import ml_dtypes
from contextlib import ExitStack
import concourse.bass as bass
import concourse.mybir as mybir
from concourse.bass_utils import run_bass_kernel_spmd

F32 = mybir.dt.float32
BF16 = mybir.dt.bfloat16
AF = mybir.ActivationFunctionType
ALU = mybir.AluOpType
AX = mybir.AxisListType

D = 1024
KC = 8
FH = 2816
NJ = 22
CTX = 256
EPS = 1e-6
N_CORES = 8
SEQ = 16384

ENG_ATTR = {"pe": "tensor", "act": "scalar", "dve": "vector", "pool": "gpsimd", "sp": "sync"}
SAME_ENGINE_SYNC = True
N_DMA_SEMS = 8


class Op:
    __slots__ = ("eng", "fn", "reads", "writes", "dma", "idx", "deps", "marked", "ev", "pre")

    def __init__(self, eng, fn, reads, writes, dma):
        self.eng, self.fn, self.reads, self.writes, self.dma = eng, fn, tuple(reads), tuple(writes), dma
        self.deps = set()
        self.marked = False
        self.ev = None
        self.pre = None


class Prog:
    def __init__(self, nc, es):
        self.nc, self.es = nc, es
        self.ops = []

    def add(self, eng, fn, reads=(), writes=(), dma=False):
        op = Op(eng, fn, reads, writes, dma)
        op.idx = len(self.ops)
        self.ops.append(op)
        return op

    def dma(self, q, out, in_, reads=(), writes=(), **kw):
        return self.add(q, lambda e: e.dma_start(out=out, in_=in_, **kw), reads, writes, dma=True)

    def mm(self, out, lhsT, rhs, start, stop, reads, writes):
        return self.add("pe", lambda e: e.matmul(out, lhsT, rhs, start=start, stop=stop), reads, writes)

    def fence(self):
        self.ops.append("FENCE")

    def emit(self):
        nc, es = self.nc, self.es
        raw = self.ops
        ops = []
        fence_before = set()
        for o in raw:
            if isinstance(o, str):
                fence_before.add(len(ops))
            else:
                o.idx = len(ops)
                ops.append(o)
        last_w = {}
        readers = {}
        last_compute = {}
        dma_slot_last = {}
        dcount0 = {}
        fence_deps = set()
        fence_pending = set()
        for op in ops:
            if op.idx in fence_before:
                fence_deps = set(last_compute.values()) | set(dma_slot_last.values())
                fence_pending = set(ENG_ATTR)
            deps = set()
            if op.eng in fence_pending:
                fence_pending.discard(op.eng)
                deps |= fence_deps
            if op.dma:
                k = dcount0.get(op.eng, 0)
                dcount0[op.eng] = k + 1
                dma_slot_last[(op.eng, k % N_DMA_SEMS)] = op.idx
            else:
                last_compute[op.eng] = op.idx
            for k in op.reads:
                if k in last_w:
                    deps.add(last_w[k])
            for k in op.writes:
                if k in last_w:
                    deps.add(last_w[k])
                for r in readers.get(k, ()):
                    deps.add(r)
            deps.discard(op.idx)
            for d in deps:
                p = ops[d]
                if not p.dma and not op.dma and p.eng == op.eng:
                    if p.eng == "pe" or not SAME_ENGINE_SYNC:
                        continue
                op.deps.add(d)
                p.marked = True
            for k in op.reads:
                readers.setdefault(k, []).append(op.idx)
            for k in op.writes:
                last_w[k] = op.idx
                readers[k] = []
        eng_sem = {e: es.enter_context(nc.semaphore("sem_" + e)) for e in ENG_ATTR}
        dma_sems = {e: [es.enter_context(nc.semaphore("dsem_%s_%d" % (e, i))) for i in range(N_DMA_SEMS)]
                    for e in ("sp", "pool", "act")}
        cnt = {e: 0 for e in ENG_ATTR}
        dcount = {e: 0 for e in dma_sems}
        duse = {e: [0] * N_DMA_SEMS for e in dma_sems}
        for op in ops:
            if op.dma:
                q = op.eng
                si = dcount[q] % N_DMA_SEMS
                dcount[q] += 1
                if duse[q][si] > 0:
                    op.pre = (dma_sems[q][si], 16 * duse[q][si])
                duse[q][si] += 1
                op.ev = (dma_sems[q][si], 16 * duse[q][si])
            elif op.marked:
                cnt[op.eng] += 1
                op.ev = (eng_sem[op.eng], cnt[op.eng])
        per_eng = {e: [] for e in ENG_ATTR}
        for op in ops:
            per_eng[op.eng].append(op)
        self.n_waits = 0

        def run_engine(ename, e):
            waited = {}
            for op in per_eng[ename]:
                need = {}
                if op.pre is not None:
                    need[id(op.pre[0])] = op.pre
                for d in op.deps:
                    sem, val = ops[d].ev
                    cur = need.get(id(sem))
                    if cur is None or cur[1] < val:
                        need[id(sem)] = (sem, val)
                for sid, (sem, val) in need.items():
                    if waited.get(sid, 0) >= val:
                        continue
                    e.wait_ge(sem, val)
                    waited[sid] = val
                    self.n_waits += 1
                ins = op.fn(e)
                if op.dma:
                    ins.then_inc(op.ev[0], 16)
                elif op.marked:
                    ins.then_inc(op.ev[0], 1)

        with nc.Block() as block:
            @block.tensor
            def _(e):
                run_engine("pe", e)

            @block.scalar
            def _(e):
                run_engine("act", e)

            @block.vector
            def _(e):
                run_engine("dve", e)

            @block.gpsimd
            def _(e):
                run_engine("pool", e)

            @block.sync
            def _(e):
                run_engine("sp", e)


class Arena:
    def __init__(self, t, n):
        self.t, self.n, self.off = t, n, 0

    def reset(self):
        self.off = 0

    def take(self, *shape):
        sz = int(np.prod(shape))
        ap = self.t[:, self.off:self.off + sz]
        self.off += sz
        assert self.off <= self.n, ("arena overflow", self.off, self.n)
        if len(shape) == 2:
            return ap.rearrange("p (a b) -> p a b", a=shape[0])
        if len(shape) == 3:
            return ap.rearrange("p (a b c) -> p a b c", a=shape[0], b=shape[1])
        return ap


def col_tiles(c0, n, w=512):
    out = []
    c = c0
    while c < c0 + n:
        ww = min(w, c0 + n - c)
        out.append((c, ww))
        c += ww
    return out


ABF_N = 53888
AF_N = 6912


def build_program(TL, stage, NK=None):
    NCOL = TL + CTX
    NB = NCOL // 128
    NKT = (NK or TL) + CTX
    NKB = NKT // 128
    nc = bass.Bass("TRN2", target_bir_lowering=False)
    es = ExitStack()
    P = Prog(nc, es)

    def din(name, shape, dt=F32):
        return nc.dram_tensor(name, list(shape), dt, kind="ExternalInput").ap()

    def dout(name, shape, dt=F32):
        return nc.dram_tensor(name, list(shape), dt, kind="ExternalOutput").ap()

    def sb(name, shape, dt):
        return es.enter_context(nc.sbuf_tensor(name, list(shape), dt))

    lat_tiles = [(c, w, 0) for (c, w) in col_tiles(0, TL)]
    ctx_tiles = [(c, w, 1) for (c, w) in col_tiles(TL, CTX)]
    all_tiles = lat_tiles + ctx_tiles
    hkeys = [("hT", kc, c0) for kc in range(KC) for (c0, w, s) in all_tiles]

    hT = sb("hT", [128, KC, NCOL], F32)
    abf_t = sb("abf", [128, ABF_N], BF16)
    af_t = sb("af", [128, AF_N], F32)
    ABF = Arena(abf_t, ABF_N)
    AFA = Arena(af_t, AF_N)
    onesb = sb("onesb_s", [128, 128], BF16)
    modT = sb("modT", [128, 2, 72, 2], F32)
    normg_s = sb("normg_s", [128, 2, 3, KC], F32)
    Avec = sb("Avec", [128, 2, 3, KC, 2], F32)
    Gvec = sb("Gvec", [128, 2, 3, KC, 2], F32)
    pbank = [es.enter_context(nc.psum_tensor("pb%d" % i, [128, 512], F32)) for i in range(7)]
    if stage in ("C", "D"):
        pbf = es.enter_context(nc.psum_tensor("pb7", [128, 1024], BF16))
    else:
        pbank.append(es.enter_context(nc.psum_tensor("pb7", [128, 512], F32)))
    outs_pending = []

    onesb_d = din("onesb", [128, 128])
    normg = din("normg", [2, 128, 3, KC])
    P.dma("pool", onesb[:], onesb_d[:, :], writes=["onesb"])
    P.dma("sp", normg_s[:], normg.rearrange("l p i k -> p l i k"), writes=["normg_s"])

    def compute_mod():
        cvec = din("cvec", [128, KC, 2])
        modw = din("modw", [2, 36, 128, KC * 256])
        modb = din("modb", [2, 128, 72])
        AFA.reset()
        cs = AFA.take(KC, 2)
        sT = AFA.take(KC, 2)
        modb_s = AFA.take(2, 72)
        modwb = [AFA.take(KC, 256) for _ in range(2)]
        P.dma("sp", cs, cvec[:, :, :], writes=["cs"])
        P.dma("sp", modb_s, modb.rearrange("l p c -> p l c"), writes=["modb_s"])
        P.add("act", lambda e: e.activation(sT, cs, AF.Silu), reads=["cs"], writes=["sT"])
        for l in range(2):
            mp = pbank[7]
            for pc in range(36):
                slot = (l * 36 + pc) % 2
                P.dma("sp", modwb[slot], modw[l, pc].rearrange("p (k f) -> p k f", k=KC),
                      writes=[("modwb", slot)])
                for fc in range(2):
                    ch = pc * 2 + fc
                    for kc in range(KC):
                        P.mm(mp[:, ch * 2:ch * 2 + 2], modwb[slot][:, kc, fc * 128:(fc + 1) * 128], sT[:, kc, :],
                             kc == 0, kc == KC - 1, reads=[("modwb", slot), "sT"], writes=[("pb", 7)])
            P.add("dve", lambda e, l=l, mp=mp: e.tensor_tensor(
                out=modT[:, l], in0=mp[:, 0:144].rearrange("p (c s) -> p c s", s=2),
                in1=modb_s[:, l].unsqueeze(2).to_broadcast([128, 72, 2]), op=ALU.add),
                reads=[("pb", 7), "modb_s"], writes=["modT"])

    def derive_mod():
        for l in range(2):
            for i in range(3):
                for s in range(2):
                    sc = modT[:, l, (3 * i + 1) * 8:(3 * i + 2) * 8, s]
                    gt = modT[:, l, (3 * i + 2) * 8:(3 * i + 3) * 8, s]
                    P.add("dve", lambda e, l=l, i=i, s=s, sc=sc: e.scalar_tensor_tensor(
                        out=Avec[:, l, i, :, s], in0=sc, scalar=1.0, in1=normg_s[:, l, i, :],
                        op0=ALU.add, op1=ALU.mult), reads=["modT", "normg_s"], writes=["Avec"])
                    P.add("dve", lambda e, l=l, i=i, s=s, gt=gt: e.tensor_scalar(
                        out=Gvec[:, l, i, :, s], in0=gt, scalar1=(1.0 if i == 1 else 0.5), scalar2=None,
                        op0=ALU.mult), reads=["modT"], writes=["Gvec"])

    def Bvec(l, i, kc, s):
        return modT[:, l, 3 * i * 8 + kc, s:s + 1]

    state = {"tmp": 0, "sg": 0, "pb_g": 0, "w13": 0, "w2": 0}

    def modulate(l, i, tiles, zT, zbase, sq, rstd, tmpA):
        for (c0, w, s) in tiles:
            ssb = pbank[6]
            for kc in range(KC):
                P.add("act", lambda e, kc=kc, c0=c0, w=w: e.activation(sq[:, kc, 0:w], hT[:, kc, c0:c0 + w], AF.Square),
                      reads=[("hT", kc, c0)], writes=[("sq", kc)])
            for kc in range(KC):
                P.mm(ssb[:, 0:w], onesb[:], sq[:, kc, 0:w], kc == 0, kc == KC - 1,
                     reads=["onesb", ("sq", kc)], writes=[("pb", 6)])
            P.add("act", lambda e, w=w, ssb=ssb: e.activation(rstd[:, 0:w], ssb[:, 0:w], AF.Sqrt, bias=EPS, scale=1.0 / D),
                  reads=[("pb", 6)], writes=["rstd"])
            P.add("dve", lambda e, w=w: e.reciprocal(rstd[:, 0:w], rstd[:, 0:w]), reads=["rstd"], writes=["rstd"])
            for kc in range(KC):
                ti = state["tmp"] % 2
                state["tmp"] += 1
                P.add("dve", lambda e, kc=kc, c0=c0, w=w, s=s, ti=ti: e.scalar_tensor_tensor(
                    out=tmpA[ti][:, 0:w], in0=hT[:, kc, c0:c0 + w], scalar=Avec[:, l, i, kc, s:s + 1],
                    in1=rstd[:, 0:w], op0=ALU.mult, op1=ALU.mult),
                    reads=[("hT", kc, c0), "Avec", "rstd"], writes=[("tmpA", ti)])
                P.add("act", lambda e, kc=kc, c0=c0, w=w, s=s, ti=ti: e.activation(
                    zT[:, kc, c0 - zbase:c0 - zbase + w], tmpA[ti][:, 0:w], AF.Identity,
                    bias=Bvec(l, i, kc, s), scale=1.0),
                    reads=[("tmpA", ti), "modT"], writes=[("zT", kc, c0 - zbase)])

    w13 = din("w13", [4, NJ, 128, KC * 2 * 128]) if stage in ("A", "B", "D", "ffn_test") else None
    w2 = din("w2", [4, KC, 128, NJ * 128]) if w13 is not None else None

    def ffn(l, f, with_ctx=True):
        P.fence()
        ABF.reset()
        AFA.reset()
        fi = l * 2 + f
        ni = 0 if f == 0 else 2
        GMAX = 1280 if TL >= 1024 else NCOL
        zT = ABF.take(KC, GMAX)
        hid = ABF.take(NJ, GMAX)
        sq = ABF.take(KC, 512)
        w13b = [ABF.take(KC, 2, 128) for _ in range(2)]
        w2b = [ABF.take(NJ, 128) for _ in range(2)]
        rstd = AFA.take(512)
        tmpA = [AFA.take(512) for _ in range(2)]
        sg = [AFA.take(512) for _ in range(2)]
        ctxt = ctx_tiles if with_ctx else []
        if TL >= 1024:
            half = len(lat_tiles) // 2
            groups = [lat_tiles[:half], lat_tiles[half:] + ctxt]
        else:
            groups = [lat_tiles + ctxt]
        for grp in groups:
            zbase = grp[0][0]
            modulate(l, ni, grp, zT, zbase, sq, rstd, tmpA)
            for j in range(NJ):
                slot = state["w13"] % 2
                state["w13"] += 1
                P.dma("pool", w13b[slot], w13[fi, j].rearrange("p (k g c) -> p k g c", k=KC, g=2),
                      writes=[("w13b", slot)])
                for (c0, w, s) in grp:
                    zc = c0 - zbase
                    b = state["pb_g"] % 2
                    state["pb_g"] += 1
                    gp, up = pbank[b], pbank[2 + b]
                    for kc in range(KC):
                        P.mm(gp[:, 0:w], w13b[slot][:, kc, 0, :], zT[:, kc, zc:zc + w], kc == 0, kc == KC - 1,
                             reads=[("w13b", slot), ("zT", kc, zc)], writes=[("pb", b)])
                    for kc in range(KC):
                        P.mm(up[:, 0:w], w13b[slot][:, kc, 1, :], zT[:, kc, zc:zc + w], kc == 0, kc == KC - 1,
                             reads=[("w13b", slot), ("zT", kc, zc)], writes=[("pb", 2 + b)])
                    si = state["sg"] % 2
                    state["sg"] += 1
                    P.add("act", lambda e, w=w, gp=gp, si=si: e.activation(sg[si][:, 0:w], gp[:, 0:w], AF.Silu),
                          reads=[("pb", b)], writes=[("sg", si)])
                    P.add("dve", lambda e, w=w, up=up, si=si, j=j, zc=zc: e.tensor_tensor(
                        out=hid[:, j, zc:zc + w], in0=sg[si][:, 0:w], in1=up[:, 0:w], op=ALU.mult),
                        reads=[("sg", si), ("pb", 2 + b)], writes=[("hid", j, zc)])
            for dc in range(KC):
                slot = state["w2"] % 2
                state["w2"] += 1
                P.dma("pool", w2b[slot], w2[fi, dc].rearrange("p (j c) -> p j c", j=NJ),
                      writes=[("w2b", slot)])
                for (c0, w, s) in grp:
                    zc = c0 - zbase
                    b = 4 + state["pb_g"] % 2
                    state["pb_g"] += 1
                    op_ = pbank[b]
                    for j in range(NJ):
                        P.mm(op_[:, 0:w], w2b[slot][:, j, :], hid[:, j, zc:zc + w], j == 0, j == NJ - 1,
                             reads=[("w2b", slot), ("hid", j, zc)], writes=[("pb", b)])
                    P.add("dve", lambda e, w=w, op_=op_, dc=dc, c0=c0, s=s: e.scalar_tensor_tensor(
                        out=hT[:, dc, c0:c0 + w], in0=op_[:, 0:w], scalar=Gvec[:, l, ni, dc, s:s + 1],
                        in1=hT[:, dc, c0:c0 + w], op0=ALU.mult, op1=ALU.add),
                        reads=[("pb", b), "Gvec", ("hT", dc, c0)], writes=[("hT", dc, c0)])

    def store(q, dram_ap, sb_ap, reads):
        op = P.dma(q, dram_ap, sb_ap, reads=reads, writes=[("out", len(outs_pending))])
        outs_pending.append(("out", len(outs_pending)))
        return op

    def finish():
        P.add("sp", lambda e: e.nop(), reads=list(outs_pending), writes=["done"])
        P.emit()

    def mixer0_project():
        winfm_d = din("winfm", [128, KC * 1280])
        wintm_d = din("wintm", [128, KC * 640])
        wsT_d = din("wsT", [128, 8 * 128])
        bsrow_d = din("bsrow", [1, 8 * 128])
        gng_d = din("gng_rep", [128, 512])
        gqk_d = din("gqk_pp", [128, 2])
        blk64_d = din("blk64", [128, 128])
        rrot_d = din("rrot", [128, 128])
        cos_d = din("cosT", [128, TL])
        sin_d = din("sinT", [128, TL])
        qT_o = dout("qT", [128, 4, NCOL], BF16)
        kT_o = dout("kT", [128, 2, NCOL], BF16)
        va_o = dout("vaug", [128, NB, 2, 65], BF16)
        gT_o = dout("gT", [128, 4, NCOL], BF16)
        P.fence()
        ABF.reset()
        AFA.reset()
        winfm = ABF.take(KC, 1280)
        wintm = ABF.take(KC, 640)
        wsT = ABF.take(8, 128)
        blk64 = ABF.take(128)
        rrot = ABF.take(128)
        zT = ABF.take(KC, 512)
        sq = ABF.take(KC, 512)
        uT = ABF.take(4, 512)
        sqh = ABF.take(512)
        qnb = ABF.take(512)
        vnb = ABF.take(512)
        qkT = ABF.take(6, NCOL)
        gT = ABF.take(4, NCOL)
        vaug = ABF.take(NB, 2, 65)
        rstd = AFA.take(512)
        tmpA = [AFA.take(512) for _ in range(2)]
        rs = AFA.take(512)
        qn = AFA.take(512)
        t1 = AFA.take(512)
        vg = AFA.take(512)
        sqv = AFA.take(512)
        gng = AFA.take(512)
        cosb = AFA.take(512)
        sinb = AFA.take(512)
        ssg = AFA.take(8)
        gqk = AFA.take(2)
        onesrow = AFA.take(64)
        bsrow = AFA.take(8, 128)
        P.dma("pool", winfm, winfm_d.rearrange("p (k f) -> p k f", k=KC), writes=["winfm"])
        P.dma("pool", wintm, wintm_d.rearrange("p (k f) -> p k f", k=KC), writes=["wintm"])
        P.dma("pool", wsT, wsT_d.rearrange("p (g t) -> p g t", g=8), writes=["wsT"])
        P.dma("pool", blk64, blk64_d[:, :], writes=["blk64"])
        P.dma("pool", rrot, rrot_d[:, :], writes=["rrot"])
        P.dma("sp", gng, gng_d[:, :], writes=["gng"])
        P.dma("sp", gqk, gqk_d[:, :], writes=["gqk"])
        P.dma("sp", bsrow[0:1], bsrow_d.rearrange("o (g t) -> o g t", g=8), writes=["bsrow"])
        P.add("dve", lambda e: e.memset(onesrow[0:1, :], 1.0), writes=["onesrow"])
        P.add("pool", lambda e: e.memset(vaug[:, :, :, 64:65], 1.0), writes=["vaug"])
        for (c0, w, s) in all_tiles:
            modulate(0, 1, [(c0, w, s)], zT, c0, sq, rstd, tmpA)
            zr = [("zT", kc, 0) for kc in range(KC)]
            if s == 0:
                P.dma("sp", cosb[:, 0:w], cos_d[:, c0:c0 + w], writes=["cosb"])
                P.dma("sp", sinb[:, 0:w], sin_d[:, c0:c0 + w], writes=["sinb"])
            for oc in range(10):
                b = oc % 2
                pp = pbank[b]
                for kc in range(KC):
                    P.mm(pp[:, 0:w], winfm[:, kc, oc * 128:(oc + 1) * 128], zT[:, kc, 0:w], kc == 0, kc == KC - 1,
                         reads=["winfm", ("zT", kc, 0)], writes=[("pb", b)])
                if oc < 4:
                    P.add("act", lambda e, pp=pp, oc=oc, w=w: e.activation(uT[:, oc, 0:w], pp[:, 0:w], AF.Gelu_apprx_tanh),
                          reads=[("pb", b)], writes=[("uT", oc)])
                    continue
                gi = 0 if oc < 8 else 1
                P.add("act", lambda e, pp=pp, w=w: e.activation(sqh[:, 0:w], pp[:, 0:w], AF.Square),
                      reads=[("pb", b)], writes=["sqh"])
                hp = pbank[2]
                P.mm(hp[:, 0:w], blk64, sqh[:, 0:w], True, True, reads=["blk64", "sqh"], writes=[("pb", 2)])
                P.add("act", lambda e, hp=hp, w=w: e.activation(rs[:, 0:w], hp[:, 0:w], AF.Sqrt, bias=EPS, scale=1.0 / 64),
                      reads=[("pb", 2)], writes=["rs"])
                P.add("dve", lambda e, w=w: e.reciprocal(rs[:, 0:w], rs[:, 0:w]), reads=["rs"], writes=["rs"])
                dst = qkT[:, oc - 4, c0:c0 + w]
                if s == 1:
                    P.add("dve", lambda e, pp=pp, w=w, gi=gi, dst=dst: e.scalar_tensor_tensor(
                        out=dst, in0=pp[:, 0:w], scalar=gqk[:, gi:gi + 1], in1=rs[:, 0:w], op0=ALU.mult, op1=ALU.mult),
                        reads=[("pb", b), "gqk", "rs"], writes=[("qkT", oc, c0)])
                    continue
                P.add("dve", lambda e, pp=pp, w=w, gi=gi: e.scalar_tensor_tensor(
                    out=qn[:, 0:w], in0=pp[:, 0:w], scalar=gqk[:, gi:gi + 1], in1=rs[:, 0:w], op0=ALU.mult, op1=ALU.mult),
                    reads=[("pb", b), "gqk", "rs"], writes=["qn"])
                P.add("act", lambda e, w=w: e.copy(qnb[:, 0:w], qn[:, 0:w]), reads=["qn"], writes=["qnb"])
                rp = pbank[3]
                P.mm(rp[:, 0:w], rrot, qnb[:, 0:w], True, True, reads=["rrot", "qnb"], writes=[("pb", 3)])
                P.add("pool", lambda e, w=w: e.tensor_tensor(out=t1[:, 0:w], in0=qn[:, 0:w], in1=cosb[:, 0:w], op=ALU.mult),
                      reads=["qn", "cosb"], writes=["t1"])
                P.add("dve", lambda e, rp=rp, w=w: e.tensor_tensor(out=qn[:, 0:w], in0=rp[:, 0:w], in1=sinb[:, 0:w], op=ALU.mult),
                      reads=[("pb", 3), "sinb", "qn"], writes=["qn"])
                P.add("dve", lambda e, w=w, dst=dst: e.tensor_tensor(out=dst, in0=qn[:, 0:w], in1=t1[:, 0:w], op=ALU.add),
                      reads=["qn", "t1"], writes=[("qkT", oc, c0)])
            for tb in range(w // 128):
                blk = (c0 + tb * 128) // 128
                vp = pbank[4 + (blk % 2)]
                vb = 4 + (blk % 2)
                for kc in range(KC):
                    P.mm(vp[:, 0:512], zT[:, kc, tb * 128:(tb + 1) * 128], wintm[:, kc, 0:512], kc == 0, kc == KC - 1,
                         reads=["wintm", ("zT", kc, 0)], writes=[("pb", vb)])
                vap = pbank[6]
                for kc in range(KC):
                    P.mm(vap[:, 0:128], zT[:, kc, tb * 128:(tb + 1) * 128], wintm[:, kc, 512:640], kc == 0, kc == KC - 1,
                         reads=["wintm", ("zT", kc, 0)], writes=[("pb", 6)])
                P.add("act", lambda e, vap=vap, blk=blk: e.copy(
                    vaug[:, blk, :, 0:64], vap[:, 0:128].rearrange("p (g d) -> p g d", g=2)),
                    reads=[("pb", 6)], writes=[("vaug", blk)])
                P.add("act", lambda e, vp=vp: e.activation(vg[:, :], vp[:, :], AF.Gelu_apprx_tanh),
                      reads=[("pb", vb)], writes=["vg"])
                P.add("pool", lambda e: e.tensor_tensor(out=sqv[:, :], in0=vg[:, :], in1=vg[:, :], op=ALU.mult),
                      reads=["vg"], writes=["sqv"])
                P.add("dve", lambda e: e.tensor_reduce(out=ssg[:, :], in_=sqv.rearrange("p (g d) -> p g d", g=8),
                                                       axis=AX.X, op=ALU.add), reads=["sqv"], writes=["ssg"])
                P.add("act", lambda e: e.activation(ssg[:, :], ssg[:, :], AF.Sqrt, bias=EPS, scale=1.0 / 64),
                      reads=["ssg"], writes=["ssg"])
                P.add("dve", lambda e: e.reciprocal(ssg[:, :], ssg[:, :]), reads=["ssg"], writes=["ssg"])
                P.add("dve", lambda e: e.tensor_tensor(
                    out=sqv.rearrange("p (g d) -> p g d", g=8), in0=vg.rearrange("p (g d) -> p g d", g=8),
                    in1=ssg.unsqueeze(2).to_broadcast([128, 8, 64]), op=ALU.mult),
                    reads=["vg", "ssg", "sqv"], writes=["sqv"])
                P.add("pool", lambda e: e.tensor_tensor(out=vnb[:, :], in0=sqv[:, :], in1=gng[:, :], op=ALU.mult),
                      reads=["sqv", "gng"], writes=["vnb"])
                sp_ = pbank[7]
                for c in range(4):
                    for gg in range(2):
                        g = 2 * c + gg
                        o_ = sp_[gg * 64:(gg + 1) * 64, c * 128:(c + 1) * 128]
                        P.mm(o_, vnb[:, g * 64:(g + 1) * 64], wsT[:, g, :], True, False,
                             reads=["vnb", "wsT"], writes=[("pb", 7)])
                        P.mm(o_, onesrow[0:1, 0:64], bsrow[0:1, g, :], False, True,
                             reads=["onesrow", "bsrow"], writes=[("pb", 7)])
                cb = c0 + tb * 128
                P.add("dve", lambda e, sp_=sp_, tb=tb, cb=cb: e.tensor_tensor(
                    out=gT[:, :, cb:cb + 128], in0=uT[:, :, tb * 128:(tb + 1) * 128],
                    in1=sp_[:, :].rearrange("p (c t) -> p c t", c=4), op=ALU.mult),
                    reads=[("pb", 7)] + [("uT", oc) for oc in range(4)], writes=[("gT", c0)])
        allq = [("qkT", oc, c0) for oc in range(4, 10) for (c0, w, s) in all_tiles]
        store("sp", qT_o[:, :, :], qkT[:, 0:4, :], allq)
        store("sp", kT_o[:, :, :], qkT[:, 4:6, :], allq)
        store("sp", gT_o[:, :, :], gT, [("gT", c0) for (c0, w, s) in all_tiles])
        store("sp", va_o[:, :, :, :], vaug, ["vaug"] + [("vaug", b) for b in range(NB)])

    def mixer0_attend():
        qT_d = din("qT", [128, 4, NCOL], BF16)
        gT_d = din("gT", [128, 4, NCOL], BF16)
        kall_d = din("kT_all", [128, 2, NKT], BF16)
        vall_d = din("vaug_all", [128, NKB, 2, 65], BF16)
        gqkrep_d = din("gqk_rep", [128, 2, 64])
        wog_d = din("wo_g", [128, 4 * 1024])
        woa_d = din("wo_a", [64, 8 * 1024])
        P.fence()
        ABF.reset()
        AFA.reset()
        qT = ABF.take(4, NCOL)
        attnT = ABF.take(8, NCOL)
        KSB = 8
        kbuf = [ABF.take(KSB * 128) for _ in range(2)]
        vbuf = [ABF.take(KSB, 65) for _ in range(2)]
        pT = [ABF.take(512) for _ in range(3)]
        gqkrep = AFA.take(2, 64)
        mx = AFA.take(2)
        negB = AFA.take(1)
        accs = AFA.take(512)
        rec = AFA.take(512)
        onesf = AFA.take(64)
        P.dma("sp", qT, qT_d[:, :, :], writes=["qT"])
        P.dma("sp", gqkrep, gqkrep_d[:, :, :], writes=["gqkrep"])
        P.add("dve", lambda e: e.memset(onesf[:, :], 1.0), writes=["onesf"])
        P.add("dve", lambda e: e.tensor_reduce(out=mx[:, :], in_=gqkrep, axis=AX.X, op=ALU.max, apply_absolute_value=True),
              reads=["gqkrep"], writes=["mx"])
        P.add("dve", lambda e: e.scalar_tensor_tensor(out=negB[:, :], in0=mx[:, 0:1], scalar=-8.0, in1=mx[:, 1:2],
                                                      op0=ALU.mult, op1=ALU.mult), reads=["mx"], writes=["negB"])
        st = {"kv": 0, "pt": 0, "sc": 0}
        for (c0, w, s) in all_tiles:
            if s == 0:
                kblocks = list(range(NKB))
            else:
                kblocks = list(range(NKB - CTX // 128, NKB))
            sbs = [kblocks[i:i + KSB] for i in range(0, len(kblocks), KSB)]
            for g in range(2):
                for si, sblk in enumerate(sbs):
                    slot = st["kv"] % 2
                    st["kv"] += 1
                    nb = len(sblk)
                    k0 = sblk[0] * 128
                    P.dma("sp", kbuf[slot][:, 0:nb * 128], kall_d[:, g, k0:k0 + nb * 128], writes=[("kbuf", slot)])
                    P.dma("sp", vbuf[slot][:, 0:nb, :], vall_d[:, sblk[0]:sblk[0] + nb, g, :], writes=[("vbuf", slot)])
                    for bi, kb in enumerate(sblk):
                        first = (si == 0 and bi == 0)
                        last = (si == len(sbs) - 1 and bi == nb - 1)
                        for hh in range(4):
                            h = 4 * g + hh
                            off = 64 * (h % 2)
                            scb = 4 + st["sc"] % 4
                            st["sc"] += 1
                            sp_ = pbank[scb]
                            P.mm(sp_[:, 0:w], kbuf[slot][off:off + 64, bi * 128:(bi + 1) * 128],
                                 qT[off:off + 64, h // 2, c0:c0 + w], True, True,
                                 reads=[("kbuf", slot), "qT"], writes=[("pb", scb)])
                            pi = st["pt"] % 3
                            st["pt"] += 1
                            P.add("act", lambda e, sp_=sp_, w=w, pi=pi: e.activation(
                                pT[pi][:, 0:w], sp_[:, 0:w], AF.Exp, bias=negB[:, 0:1], scale=0.125),
                                reads=[("pb", scb), "negB"], writes=[("pT", pi)])
                            P.mm(pbank[hh][0:65, 0:w], vbuf[slot][:, bi, :], pT[pi][:, 0:w], first, last,
                                 reads=[("vbuf", slot), ("pT", pi)], writes=[("pb", hh)])
                for hh in range(4):
                    h = 4 * g + hh
                    P.add("act", lambda e, hh=hh, w=w: e.copy(accs[0:65, 0:w], pbank[hh][0:65, 0:w]),
                          reads=[("pb", hh)], writes=["accs"])
                    P.add("dve", lambda e, w=w: e.reciprocal(rec[64:65, 0:w], accs[64:65, 0:w]),
                          reads=["accs"], writes=["rec"])
                    bp = pbank[4 + st["sc"] % 4]
                    bpk = ("pb", 4 + st["sc"] % 4)
                    st["sc"] += 1
                    P.mm(bp[0:64, 0:w], onesf[64:65, 0:64], rec[64:65, 0:w], True, True,
                         reads=["onesf", "rec"], writes=[bpk])
                    P.add("dve", lambda e, bp=bp, w=w, h=h, c0=c0: e.tensor_tensor(
                        out=attnT[0:64, h, c0:c0 + w], in0=accs[0:64, 0:w], in1=bp[0:64, 0:w], op=ALU.mult),
                        reads=["accs", bpk], writes=[("attnT", h, c0)])
        P.fence()
        gT = ABF.take(4, NCOL)
        wog = ABF.take(4, 1024)
        woa = ABF.take(8, 1024)
        P.dma("sp", gT, gT_d[:, :, :], writes=["gT"])
        P.dma("pool", wog, wog_d.rearrange("p (c f) -> p c f", c=4), writes=["wog"])
        P.dma("pool", woa[0:64], woa_d.rearrange("p (c f) -> p c f", c=8), writes=["woa"])
        for (c0, w, s) in all_tiles:
            for oc in range(KC):
                b = oc % 2
                yp = pbank[b]
                for c in range(4):
                    P.mm(yp[:, 0:w], wog[:, c, oc * 128:(oc + 1) * 128], gT[:, c, c0:c0 + w], c == 0, False,
                         reads=["wog", "gT"], writes=[("pb", b)])
                for h in range(8):
                    P.mm(yp[:, 0:w], woa[0:64, h, oc * 128:(oc + 1) * 128], attnT[0:64, h, c0:c0 + w], False, h == 7,
                         reads=["woa", ("attnT", h, c0)], writes=[("pb", b)])
                P.add("dve", lambda e, yp=yp, w=w, oc=oc, c0=c0, s=s: e.scalar_tensor_tensor(
                    out=hT[:, oc, c0:c0 + w], in0=yp[:, 0:w], scalar=Gvec[:, 0, 1, oc, s:s + 1],
                    in1=hT[:, oc, c0:c0 + w], op0=ALU.mult, op1=ALU.add),
                    reads=[("pb", b), "Gvec", ("hT", oc, c0)], writes=[("hT", oc, c0)])


    def mixer1_project():
        w1fm_d = din("w1fm", [128, KC * 2048])
        w1tm_d = din("w1tm", [128, KC * 1056])
        gb_d = din("gateb_rep", [128, 32])
        pqk_o = dout("pqkT", [128, KC, NCOL])
        sig_o = dout("sigoT", [128, KC, NCOL], BF16)
        va_o = dout("vaug1", [128, NB, 8, 129], BF16)
        gt_o = dout("gates", [128, NB, 32])
        P.fence()
        ABF.reset()
        AFA.reset()
        w1fm = ABF.take(KC, 2048)
        w1tm = ABF.take(KC, 1056)
        zT = ABF.take(KC, 512)
        sq = ABF.take(KC, 512)
        sigo = [ABF.take(KC, 512) for _ in range(2)]
        vaug = [ABF.take(4, 8, 129) for _ in range(2)]
        rstd = AFA.take(512)
        tmpA = [AFA.take(512) for _ in range(2)]
        pq = [AFA.take(512) for _ in range(4)]
        gb = AFA.take(32)
        gts = [AFA.take(4, 32) for _ in range(2)]
        P.dma("pool", w1fm, w1fm_d.rearrange("p (k f) -> p k f", k=KC), writes=["w1fm"])
        P.dma("pool", w1tm, w1tm_d.rearrange("p (k f) -> p k f", k=KC), writes=["w1tm"])
        P.dma("sp", gb, gb_d[:, :], writes=["gb"])
        for i in range(2):
            P.add("pool", lambda e, i=i: e.memset(vaug[i][:, :, :, 128:129], 1.0), writes=[("vaug1", i)])
        pqi = 0
        for ti, (c0, w, s) in enumerate(all_tiles):
            modulate(1, 1, [(c0, w, s)], zT, c0, sq, rstd, tmpA)
            tsl = ti % 2
            for oc in range(16):
                b = oc % 2
                pp = pbank[b]
                for kc in range(KC):
                    P.mm(pp[:, 0:w], w1fm[:, kc, oc * 128:(oc + 1) * 128], zT[:, kc, 0:w], kc == 0, kc == KC - 1,
                         reads=["w1fm", ("zT", kc, 0)], writes=[("pb", b)])
                if oc < 8:
                    pi = pqi % 4
                    pqi += 1
                    P.add("act", lambda e, pp=pp, w=w, pi=pi: e.copy(pq[pi][:, 0:w], pp[:, 0:w]),
                          reads=[("pb", b)], writes=[("pq", pi)])
                    P.dma("sp", pqk_o[:, oc, c0:c0 + w], pq[pi][:, 0:w], reads=[("pq", pi)], writes=[("pqo", pi)])
                    outs_pending.append(("pqo", pi))
                else:
                    P.add("act", lambda e, pp=pp, w=w, oc=oc, tsl=tsl: e.activation(
                        sigo[tsl][:, oc - 8, 0:w], pp[:, 0:w], AF.Sigmoid), reads=[("pb", b)], writes=[("sigo", tsl)])
            P.dma("sp", sig_o[:, :, c0:c0 + w], sigo[tsl][:, :, 0:w], reads=[("sigo", tsl)], writes=[("sigo_o", tsl)])
            outs_pending.append(("sigo_o", tsl))
            nbk = w // 128
            for tb in range(nbk):
                for half in range(2):
                    vb = 2 + half
                    vp = pbank[vb]
                    for kc in range(KC):
                        P.mm(vp[:, 0:512], zT[:, kc, tb * 128:(tb + 1) * 128], w1tm[:, kc, half * 512:(half + 1) * 512],
                             kc == 0, kc == KC - 1, reads=["w1tm", ("zT", kc, 0)], writes=[("pb", vb)])
                    P.add("act", lambda e, vp=vp, tb=tb, half=half, tsl=tsl: e.copy(
                        vaug[tsl][:, tb, half * 4:(half + 1) * 4, 0:128], vp[:, :].rearrange("p (h d) -> p h d", h=4)),
                        reads=[("pb", vb)], writes=[("vaug1", tsl)])
                gp_ = pbank[4]
                for kc in range(KC):
                    P.mm(gp_[:, 0:32], zT[:, kc, tb * 128:(tb + 1) * 128], w1tm[:, kc, 1024:1056], kc == 0, kc == KC - 1,
                         reads=["w1tm", ("zT", kc, 0)], writes=[("pb", 4)])
                P.add("dve", lambda e, gp_=gp_, tb=tb, tsl=tsl: e.tensor_tensor(
                    out=gts[tsl][:, tb, :], in0=gp_[:, 0:32], in1=gb[:, :], op=ALU.add),
                    reads=[("pb", 4), "gb"], writes=[("gts", tsl)])
            b0 = c0 // 128
            P.dma("sp", va_o[:, b0:b0 + nbk], vaug[tsl][:, 0:nbk], reads=[("vaug1", tsl)], writes=[("va_o", tsl)])
            outs_pending.append(("va_o", tsl))
            P.dma("sp", gt_o[:, b0:b0 + nbk, :], gts[tsl][:, 0:nbk, :], reads=[("gts", tsl)], writes=[("gt_o", tsl)])
            outs_pending.append(("gt_o", tsl))

    NBL = TL // 128

    def state_scan(S, blocks_by_dir, k_src, va_d, garr, ident_b, store_to, tag):
        kc_b = [ABF.take(4, 128) for _ in range(2)]
        va_b = [ABF.take(8, 129) for _ in range(2)]
        kw = [ABF.take(8, 64) for _ in range(2)]
        order = []
        nb = len(blocks_by_dir[0])
        for i in range(nb):
            order.append((0, blocks_by_dir[0][i]))
            order.append((1, blocks_by_dir[1][i]))
        li = 0
        for (d, blk) in order:
            sl = li % 2
            li += 1
            P.dma("sp", va_b[sl], va_d[:, blk], writes=[(tag + "va", sl)])
            if k_src[0] == "dram":
                P.dma("sp", kc_b[sl], k_src[1][:, 4:8, blk * 128:(blk + 1) * 128], writes=[(tag + "kc", sl)])
                for pr in range(4):
                    P.add("pe", lambda e, pr=pr, sl=sl: e.transpose(pbf[:, pr * 128:(pr + 1) * 128], kc_b[sl][:, pr, :], ident_b),
                          reads=[(tag + "kc", sl), "ident_b"], writes=["pbf"])
            else:
                for pr in range(4):
                    P.add("pe", lambda e, pr=pr, blk=blk: e.transpose(
                        pbf[:, pr * 128:(pr + 1) * 128], k_src[1][:, 4 + pr, blk * 128:(blk + 1) * 128], ident_b),
                        reads=[("qk_s", 4 + pr), "ident_b"], writes=["pbf"])
            P.add("dve", lambda e, sl=sl, d=d, blk=blk: e.tensor_tensor(
                out=kw[sl], in0=pbf[:, 0:512].rearrange("p (h k) -> p h k", h=8),
                in1=garr[:, blk, 16 + d * 8:16 + (d + 1) * 8].unsqueeze(2).to_broadcast([128, 8, 64]), op=ALU.mult),
                reads=["pbf", "garr"], writes=[(tag + "kw", sl)])
            for h in range(8):
                off = 64 * (h % 2)
                pr = h // 2
                bk = 4 + pr // 2
                P.mm(pbank[bk][off:off + 64, (pr % 2) * 129:(pr % 2) * 129 + 129], kw[sl][:, h, :], va_b[sl][:, h, :],
                     True, True, reads=[(tag + "kw", sl), (tag + "va", sl)], writes=[("pb", bk)])
            if store_to is not None:
                P.add("act", lambda e, d=d, blk=blk: e.copy(store_to[d][blk], S[:, d]),
                      reads=[("S", d)], writes=[("Sin", d, blk)])
            P.add("dve", lambda e, d=d, blk=blk: e.tensor_tensor(
                out=S[:, d], in0=S[:, d], in1=garr[:, blk, 48 + d * 4:48 + (d + 1) * 4].unsqueeze(2).to_broadcast([128, 4, 129]),
                op=ALU.mult), reads=[("S", d), "garr"], writes=[("S", d)])
            for hf in range(2):
                P.add("dve", lambda e, d=d, hf=hf: e.tensor_tensor(
                    out=S[:, d, 2 * hf:2 * hf + 2, :], in0=S[:, d, 2 * hf:2 * hf + 2, :],
                    in1=pbank[4 + hf][:, 0:258].rearrange("p (a b) -> p a b", a=2), op=ALU.add),
                    reads=[("S", d), ("pb", 4 + hf)], writes=[("S", d)])

    def stage_c():
        pqk_d = din("pqkT", [128, KC, NCOL])
        halo_d = din("halo", [128, KC, 2])
        convw_d = din("convw", [128, KC, 3])
        gates_d = din("gates", [128, NB, 32])
        va_d = din("vaug1", [128, NB, 8, 129], BF16)
        trif_d = din("tri_f", [128, 128])
        trib_d = din("tri_b", [128, 128])
        onesf_d = din("ones_f", [128, 128])
        identf_d = din("ident_f", [128, 128])
        qk_o = dout("qkT", [128, KC, NCOL], BF16)
        garr_o = dout("garr", [128, NB, 72])
        Lagg_o = dout("Lagg", [128, 2, 4, 129])
        Aagg_o = dout("Aagg", [128, 2, 4])
        Lctx_o = dout("Lctx", [128, 2, 4, 129])
        ABF.reset()
        AFA.reset()
        ident_b = ABF.take(128)
        qk_s = ABF.take(KC, NCOL)
        ext = [AFA.take(TL + 2)] * 2
        extc = AFA.take(CTX + 2)
        cv = AFA.take(512)
        convw = AFA.take(KC, 3)
        halo = AFA.take(KC, 2)
        gates = AFA.take(NB, 32)
        lf = AFA.take(NB, 16)
        garr = AFA.take(NB, 72)
        trif = AFA.take(128)
        trib = AFA.take(128)
        onesf = AFA.take(128)
        S = AFA.take(2, 4, 129)
        sumB = AFA.take(8)
        tmpg = AFA.take(16)
        P.dma("pool", ident_b, identf_d[:, :], writes=["ident_b"])
        P.dma("sp", convw, convw_d[:, :, :], writes=["convw"])
        P.dma("sp", halo, halo_d[:, :, :], writes=["halo"])
        P.dma("sp", gates, gates_d[:, :, :], writes=["gates"])
        P.dma("sp", trif, trif_d[:, :], writes=["trif"])
        P.dma("sp", trib, trib_d[:, :], writes=["trib"])
        P.dma("sp", onesf, onesf_d[:, :], writes=["onesf"])
        P.add("pool", lambda e: e.memset(extc[:, 0:1], 0.0), writes=["extc"])
        P.add("pool", lambda e: e.memset(extc[:, CTX + 1:CTX + 2], 0.0), writes=["extc"])
        for kc in range(KC):
            for part in range(2):
                if part == 0:
                    eb_ = ext[0]
                    ek = ("ext", 0)
                    n = TL
                    P.dma("sp", eb_[:, 1:TL + 1], pqk_d[:, kc, 0:TL], writes=[ek])
                    P.add("pool", lambda e, eb_=eb_, kc=kc: e.tensor_copy(eb_[:, 0:1], halo[:, kc, 0:1]), reads=["halo"], writes=[ek])
                    P.add("pool", lambda e, eb_=eb_, kc=kc: e.tensor_copy(eb_[:, TL + 1:TL + 2], halo[:, kc, 1:2]), reads=["halo"], writes=[ek])
                    cbase = 0
                else:
                    eb_ = extc
                    ek = "extc"
                    n = CTX
                    P.dma("sp", eb_[:, 1:CTX + 1], pqk_d[:, kc, TL:NCOL], writes=[ek])
                    cbase = TL
                for (c0, w) in col_tiles(0, n):
                    P.add("dve", lambda e, eb_=eb_, c0=c0, w=w, kc=kc: e.tensor_scalar(
                        out=cv[:, 0:w], in0=eb_[:, c0:c0 + w], scalar1=convw[:, kc, 0:1], scalar2=None, op0=ALU.mult),
                        reads=[ek, "convw"], writes=["cv"])
                    for j in (1, 2):
                        P.add("dve", lambda e, eb_=eb_, c0=c0, w=w, kc=kc, j=j: e.scalar_tensor_tensor(
                            out=cv[:, 0:w], in0=eb_[:, c0 + j:c0 + j + w], scalar=convw[:, kc, j:j + 1], in1=cv[:, 0:w],
                            op0=ALU.mult, op1=ALU.add), reads=[ek, "convw", "cv"], writes=["cv"])
                    dst = qk_s[:, kc, cbase + c0:cbase + c0 + w]
                    P.add("act", lambda e, w=w, dst=dst: e.activation(dst, cv[:, 0:w], AF.Silu),
                          reads=["cv"], writes=[("qk_s", kc)])
                    if kc < 4:
                        P.add("pool", lambda e, dst=dst: e.tensor_scalar(out=dst, in0=dst, scalar1=0.125, scalar2=None, op0=ALU.mult),
                              reads=[("qk_s", kc)], writes=[("qk_s", kc)])
        store("sp", qk_o[:, :, :], qk_s, [("qk_s", kc) for kc in range(KC)])
        g5 = gates.rearrange("p n (d a h) -> p n d a h", d=2, a=2)
        lf4 = lf.rearrange("p n (d h) -> p n d h", d=2)
        for d in range(2):
            P.add("act", lambda e, d=d: e.activation(lf4[:, :, d, :], g5[:, :, d, 1, :], AF.Exp, scale=-1.0),
                  reads=["gates"], writes=["lf"])
        P.add("act", lambda e: e.activation(lf, lf, AF.Ln, bias=1.0, scale=1.0), reads=["lf"], writes=["lf"])
        P.add("dve", lambda e: e.tensor_scalar(out=lf, in0=lf, scalar1=-1.0, scalar2=None, op0=ALU.mult), reads=["lf"], writes=["lf"])
        P.add("dve", lambda e: e.memset(sumB, 0.0), writes=["sumB"])
        for blk in range(NB):
            cp = pbank[0]
            P.mm(cp[:, 0:8], trif, lf[:, blk, 0:8], True, True, reads=["trif", "lf"], writes=[("pb", 0)])
            P.mm(cp[:, 8:16], trib, lf[:, blk, 8:16], True, True, reads=["trib", "lf"], writes=[("pb", 0)])
            P.mm(cp[:, 16:32], onesf, lf[:, blk, :], True, True, reads=["onesf", "lf"], writes=[("pb", 0)])
            lfe = lf[:, blk, :].rearrange("p (d r t) -> p d r t", d=2, t=2)
            P.mm(cp[0:64, 32:40], onesf[:, 0:64], lfe[:, :, :, 0], True, True, reads=["onesf", "lf"], writes=[("pb", 0)])
            P.mm(cp[64:128, 32:40], onesf[:, 0:64], lfe[:, :, :, 1], True, True, reads=["onesf", "lf"], writes=[("pb", 0)])
            li_v = g5[:, blk, :, 0, :]
            P.add("dve", lambda e, cp=cp, blk=blk, li_v=li_v: e.tensor_tensor(
                out=garr[:, blk, 0:16].rearrange("p (d h) -> p d h", d=2), in0=li_v,
                in1=cp[:, 0:16].rearrange("p (d h) -> p d h", d=2), op=ALU.subtract),
                reads=["gates", ("pb", 0)], writes=["garr"])
            P.add("dve", lambda e, cp=cp, blk=blk: e.tensor_tensor(out=tmpg, in0=garr[:, blk, 0:16], in1=cp[:, 16:32], op=ALU.add),
                  reads=["garr", ("pb", 0)], writes=["tmpg"])
            P.add("act", lambda e, blk=blk: e.activation(garr[:, blk, 16:32], tmpg, AF.Exp), reads=["tmpg"], writes=["garr"])
            P.add("act", lambda e, cp=cp, blk=blk: e.activation(garr[:, blk, 32:48], cp[:, 0:16], AF.Exp),
                  reads=[("pb", 0)], writes=["garr"])
            P.add("act", lambda e, cp=cp, blk=blk: e.activation(garr[:, blk, 48:56], cp[:, 32:40], AF.Exp),
                  reads=[("pb", 0)], writes=["garr"])
            if blk < NBL:
                P.add("dve", lambda e, cp=cp: e.tensor_tensor(out=sumB, in0=sumB, in1=cp[:, 32:40], op=ALU.add),
                      reads=["sumB", ("pb", 0)], writes=["sumB"])
            P.add("act", lambda e, cp=cp, blk=blk: e.copy(garr[:, blk, 56:72], cp[:, 0:16]), reads=[("pb", 0)], writes=["garr"])
        store("sp", garr_o[:, :, :], garr, ["garr"])
        P.add("act", lambda e: e.activation(sumB, sumB, AF.Exp), reads=["sumB"], writes=["sumB"])
        store("sp", Aagg_o.rearrange("p d r -> p (d r)"), sumB, ["sumB"])
        P.add("dve", lambda e: e.memset(S, 0.0), writes=[("S", 0), ("S", 1)])
        cb = list(range(NBL, NB))
        state_scan(S, [cb, cb[::-1]], ("sbuf", qk_s), va_d, garr, ident_b, None, "c")
        store("sp", Lctx_o[:, :, :, :], S, [("S", 0), ("S", 1)])
        P.fence()
        P.add("dve", lambda e: e.memset(S, 0.0), writes=[("S", 0), ("S", 1)])
        lb_ = list(range(NBL))
        state_scan(S, [lb_, lb_[::-1]], ("sbuf", qk_s), va_d, garr, ident_b, None, "l")
        store("sp", Lagg_o[:, :, :, :], S, [("S", 0), ("S", 1)])

    def stage_d():
        qk_d = din("qkT", [128, KC, NCOL], BF16)
        va_d = din("vaug1", [128, NB, 8, 129], BF16)
        sig_d = din("sigoT", [128, KC, NCOL], BF16)
        garr_d = din("garr", [128, NB, 72])
        seqA_d = din("seqA", [128, 2, 7, 4])
        seqL_d = din("seqL", [128, 2, 7, 4 * 129])
        Lctx_d = din("Lctx", [128, 2, 4, 129])
        identf_d = din("ident_f", [128, 128])
        mask_d = din("maskneg", [128, 2 * 128])
        wo1_d = din("wo1", [128, 8 * 1024])
        outg_d = din("outg", [128, 8])
        fng_d = din("fng", [128, 8])
        out_o = dout("outT", [128, KC, TL])
        ABF.reset()
        AFA.reset()
        ident_b = ABF.take(128)
        Sin = [[ABF.take(4, 129) for _ in range(NBL)] for _ in range(2)]
        hgT = ABF.take(8, TL)
        garr = AFA.take(NB, 72)
        identf = AFA.take(128)
        maskn = AFA.take(2, 128)
        S = AFA.take(2, 4, 129)
        seqA = AFA.take(2, 7, 4)
        seqLb = [AFA.take(4, 129) for _ in range(2)]
        outg = AFA.take(8)
        fng = AFA.take(8)
        P.dma("pool", ident_b, identf_d[:, :], writes=["ident_b"])
        P.dma("sp", garr, garr_d[:, :, :], writes=["garr"])
        P.dma("sp", identf, identf_d[:, :], writes=["identf"])
        P.dma("sp", maskn, mask_d.rearrange("p (d j) -> p d j", d=2), writes=["maskn"])
        P.dma("sp", seqA, seqA_d[:, :, :, :], writes=["seqA"])
        P.dma("sp", outg, outg_d[:, :], writes=["outg"])
        P.dma("sp", fng, fng_d[:, :], writes=["fng"])
        P.dma("sp", S, Lctx_d[:, :, :, :], writes=[("S", 0), ("S", 1)])
        k = 0
        for d in range(2):
            for j in range(7):
                sl = k % 2
                k += 1
                P.dma("sp", seqLb[sl], seqL_d[:, d, j].rearrange("p (r v) -> p r v", r=4), writes=[("seqLb", sl)])
                P.add("dve", lambda e, d=d, j=j: e.tensor_tensor(
                    out=S[:, d], in0=S[:, d], in1=seqA[:, d, j, :].unsqueeze(2).to_broadcast([128, 4, 129]), op=ALU.mult),
                    reads=[("S", d), "seqA"], writes=[("S", d)])
                P.add("dve", lambda e, d=d, sl=sl: e.tensor_tensor(out=S[:, d], in0=S[:, d], in1=seqLb[sl], op=ALU.add),
                      reads=[("S", d), ("seqLb", sl)], writes=[("S", d)])
        lb_ = list(range(NBL))
        state_scan(S, [lb_, lb_[::-1]], ("dram", qk_d), va_d, garr, ident_b, Sin, "l")
        P.fence()
        qc_b = [ABF.take(8, 128) for _ in range(2)]
        va_b = [ABF.take(8, 129) for _ in range(2)]
        sg_b = [ABF.take(8, 128) for _ in range(2)]
        SD = [ABF.take(128) for _ in range(2)]
        hn = ABF.take(8, 128)
        DT = [AFA.take(128) for _ in range(2)]
        bc = [AFA.take(128) for _ in range(2)]
        T1 = [AFA.take(129) for _ in range(2)]
        NUM = [AFA.take(129) for _ in range(2)]
        dn = [AFA.take(1) for _ in range(2)]
        Hacc = AFA.take(8, 128)
        sqH = AFA.take(8, 128)
        ssq = AFA.take(8)
        it = 0
        for blk in range(NBL):
            sl = blk % 2
            cs_ = slice(blk * 128, (blk + 1) * 128)
            P.dma("sp", qc_b[sl], qk_d[:, :, cs_], writes=[("qc", sl)])
            P.dma("sp", va_b[sl], va_d[:, blk], writes=[("vab", sl)])
            P.dma("sp", sg_b[sl], sig_d[:, :, cs_], writes=[("sgb", sl)])
            for d in range(2):
                for h in range(8):
                    c = d * 8 + h
                    pr, off = h // 2, 64 * (h % 2)
                    r = it % 2
                    it += 1
                    gpk = ("pb", r)
                    G = pbank[r]
                    P.add("pool", lambda e, r=r, blk=blk, c=c: e.tensor_copy(
                        bc[r], garr[:, blk, 56 + c:57 + c].to_broadcast([128, 128])), reads=["garr"], writes=[("bc", r)])
                    P.mm(G[:, 0:128], bc[r], identf, True, False, reads=[("bc", r), "identf"], writes=[gpk])
                    P.mm(G[:, 0:128], identf, maskn[:, d, :], False, True, reads=["identf", "maskn"], writes=[gpk])
                    P.add("act", lambda e, G=G, r=r, blk=blk, c=c: e.activation(
                        DT[r], G[:, 0:128], AF.Exp, bias=garr[:, blk, c:c + 1], scale=1.0),
                        reads=[gpk, "garr"], writes=[("DT", r)])
                    STp = pbank[2 + r]
                    P.mm(STp[:, 0:128], qc_b[sl][off:off + 64, 4 + pr, :], qc_b[sl][off:off + 64, pr, :], True, True,
                         reads=[("qc", sl)], writes=[("pb", 2 + r)])
                    P.add("dve", lambda e, STp=STp, r=r: e.tensor_tensor(out=SD[r], in0=STp[:, 0:128], in1=DT[r], op=ALU.mult),
                          reads=[("pb", 2 + r), ("DT", r)], writes=[("SD", r)])
                    O1 = pbank[4]
                    P.mm(O1[:, 0:129], SD[r], va_b[sl][:, h, :], True, True, reads=[("SD", r), ("vab", sl)], writes=[("pb", 4)])
                    O2 = pbank[5]
                    P.mm(O2[:, 0:129], qc_b[sl][off:off + 64, pr, :], Sin[d][blk][off:off + 64, pr, :], True, True,
                         reads=[("qc", sl), ("Sin", d, blk)], writes=[("pb", 5)])
                    P.add("act", lambda e, O2=O2, r=r, blk=blk, c=c: e.activation(
                        T1[r], O2[:, 0:129], AF.Copy, scale=garr[:, blk, 32 + c:33 + c]),
                        reads=[("pb", 5), "garr"], writes=[("T1", r)])
                    P.add("dve", lambda e, O1=O1, r=r: e.tensor_tensor(out=NUM[r], in0=T1[r], in1=O1[:, 0:129], op=ALU.add),
                          reads=[("T1", r), ("pb", 4)], writes=[("NUM", r)])
                    P.add("act", lambda e, r=r: e.activation(dn[r], NUM[r][:, 128:129], AF.Abs),
                          reads=[("NUM", r)], writes=[("dn", r)])
                    P.add("dve", lambda e, r=r: e.tensor_scalar(out=dn[r], in0=dn[r], scalar1=1.0, scalar2=None, op0=ALU.max),
                          reads=[("dn", r)], writes=[("dn", r)])
                    P.add("dve", lambda e, r=r: e.reciprocal(dn[r], dn[r]), reads=[("dn", r)], writes=[("dn", r)])
                    if d == 0:
                        P.add("dve", lambda e, r=r, h=h: e.tensor_scalar(
                            out=Hacc[:, h, :], in0=NUM[r][:, 0:128], scalar1=dn[r][:, 0:1], scalar2=None, op0=ALU.mult),
                            reads=[("NUM", r), ("dn", r)], writes=[("Hacc", h)])
                    else:
                        P.add("dve", lambda e, r=r, h=h: e.scalar_tensor_tensor(
                            out=Hacc[:, h, :], in0=NUM[r][:, 0:128], scalar=dn[r][:, 0:1], in1=Hacc[:, h, :],
                            op0=ALU.mult, op1=ALU.add), reads=[("NUM", r), ("dn", r), ("Hacc", h)], writes=[("Hacc", h)])
            hk = [("Hacc", h) for h in range(8)]
            P.add("pool", lambda e: e.tensor_tensor(out=sqH, in0=Hacc, in1=Hacc, op=ALU.mult), reads=hk, writes=["sqH"])
            P.add("dve", lambda e: e.tensor_reduce(out=ssq, in_=sqH, axis=AX.X, op=ALU.add), reads=["sqH"], writes=["ssq"])
            P.add("act", lambda e: e.activation(ssq, ssq, AF.Sqrt, bias=EPS, scale=1.0 / 128), reads=["ssq"], writes=["ssq"])
            P.add("dve", lambda e: e.reciprocal(ssq, ssq), reads=["ssq"], writes=["ssq"])
            P.add("dve", lambda e: e.tensor_tensor(out=hn, in0=Hacc, in1=ssq.unsqueeze(2).to_broadcast([128, 8, 128]), op=ALU.mult),
                  reads=hk + ["ssq"], writes=["hn"])
            for h in range(8):
                P.add("pe", lambda e, h=h: e.transpose(pbf[:, h * 128:(h + 1) * 128], hn[:, h, :], ident_b),
                      reads=["hn", "ident_b"], writes=["pbf"])
            P.add("dve", lambda e, sl=sl: e.tensor_tensor(
                out=sg_b[sl], in0=sg_b[sl], in1=outg.unsqueeze(2).to_broadcast([128, 8, 128]), op=ALU.mult),
                reads=[("sgb", sl), "outg"], writes=[("sgb", sl)])
            P.add("dve", lambda e, sl=sl, cs_=cs_: e.tensor_tensor(
                out=hgT[:, :, cs_], in0=pbf[:, :].rearrange("p (h t) -> p h t", h=8), in1=sg_b[sl], op=ALU.mult),
                reads=["pbf", ("sgb", sl)], writes=[("hgT", blk // 4)])
        P.fence()
        wo1 = ABF.take(8, 1024)
        P.dma("pool", wo1, wo1_d.rearrange("p (c f) -> p c f", c=8), writes=["wo1"])
        for ti, (c0, w, s) in enumerate(lat_tiles):
            for oc in range(KC):
                b = oc % 2
                yp = pbank[b]
                for h in range(8):
                    P.mm(yp[:, 0:w], wo1[:, h, oc * 128:(oc + 1) * 128], hgT[:, h, c0:c0 + w], h == 0, h == 7,
                         reads=["wo1", ("hgT", ti)], writes=[("pb", b)])
                P.add("dve", lambda e, yp=yp, w=w, oc=oc, c0=c0: e.scalar_tensor_tensor(
                    out=hT[:, oc, c0:c0 + w], in0=yp[:, 0:w], scalar=Gvec[:, 1, 1, oc, 0:1],
                    in1=hT[:, oc, c0:c0 + w], op0=ALU.mult, op1=ALU.add),
                    reads=[("pb", b), "Gvec", ("hT", oc, c0)], writes=[("hT", oc, c0)])
        ffn(1, 1, with_ctx=False)
        P.fence()
        ABF.reset()
        AFA.reset()
        sq = ABF.take(KC, 512)
        rstd = AFA.take(512)
        ob = [AFA.take(512) for _ in range(2)]
        fng2 = AFA.take(8)
        P.dma("sp", fng2, fng_d[:, :], writes=["fng2"])
        k = 0
        for (c0, w, s) in lat_tiles:
            ssb = pbank[6]
            for kc in range(KC):
                P.add("act", lambda e, kc=kc, c0=c0, w=w: e.activation(sq[:, kc, 0:w], hT[:, kc, c0:c0 + w], AF.Square),
                      reads=[("hT", kc, c0)], writes=[("sq", kc)])
            for kc in range(KC):
                P.mm(ssb[:, 0:w], onesb[:], sq[:, kc, 0:w], kc == 0, kc == KC - 1, reads=["onesb", ("sq", kc)], writes=[("pb", 6)])
            P.add("act", lambda e, w=w, ssb=ssb: e.activation(rstd[:, 0:w], ssb[:, 0:w], AF.Sqrt, bias=EPS, scale=1.0 / D),
                  reads=[("pb", 6)], writes=["rstd"])
            P.add("dve", lambda e, w=w: e.reciprocal(rstd[:, 0:w], rstd[:, 0:w]), reads=["rstd"], writes=["rstd"])
            for kc in range(KC):
                sl = k % 2
                k += 1
                P.add("dve", lambda e, kc=kc, c0=c0, w=w, sl=sl: e.scalar_tensor_tensor(
                    out=ob[sl][:, 0:w], in0=hT[:, kc, c0:c0 + w], scalar=fng2[:, kc:kc + 1], in1=rstd[:, 0:w],
                    op0=ALU.mult, op1=ALU.mult), reads=[("hT", kc, c0), "fng2", "rstd"], writes=[("ob", sl)])
                store("sp", out_o[:, kc, c0:c0 + w], ob[sl][:, 0:w], [("ob", sl)])

    if stage in ("A", "ffn_test"):
        xT = din("xT", [128, KC, NCOL])
        P.dma("sp", hT[:], xT[:, :, :], writes=hkeys)
        compute_mod()
        derive_mod()
        ffn(0, 0)
        hT_out = dout("hT_out", [128, KC, NCOL])
        if stage == "A":
            modT_out = dout("modT_out", [128, 2 * 72 * 2])
            mixer0_project()
            store("sp", modT_out[:, :], modT[:].rearrange("p l c s -> p (l c s)"), ["modT"])
        store("sp", hT_out[:, :, :], hT[:], hkeys)
    elif stage == "B":
        hT_in = din("hT_in", [128, KC, NCOL])
        modT_in = din("modT_in", [128, 2 * 72 * 2])
        P.dma("sp", hT[:], hT_in[:, :, :], writes=hkeys)
        P.dma("sp", modT[:].rearrange("p l c s -> p (l c s)"), modT_in[:, :], writes=["modT"])
        derive_mod()
        mixer0_attend()
        ffn(0, 1)
        ffn(1, 0)
        mixer1_project()
        hT_out = dout("hT_out", [128, KC, NCOL])
        store("sp", hT_out[:, :, :], hT[:], hkeys)
    elif stage == "C":
        stage_c()
    elif stage == "D":
        hT_in = din("hT_in", [128, KC, NCOL])
        modT_in = din("modT_in", [128, 2 * 72 * 2])
        P.dma("sp", hT[:], hT_in[:, :, :], writes=hkeys)
        P.dma("sp", modT[:].rearrange("p l c s -> p (l c s)"), modT_in[:, :], writes=["modT"])
        derive_mod()
        stage_d()
    finish()
    return nc, es


def fm(a):
    T = a.shape[0]
    return np.ascontiguousarray(a.reshape(T, KC, 128).transpose(2, 1, 0))


def vec_fm(v, nch):
    return np.ascontiguousarray(v.reshape(nch, 128).T)


def wfm(w):
    N = w.shape[1]
    return np.ascontiguousarray(w.reshape(KC, 128, N).transpose(1, 0, 2).reshape(128, KC * N))


def rope_tables(t0, TL):
    GRID_W, HD, THETA = 64, 64, 10000.0
    axis_dim = HD // 2
    inv = (THETA ** (-np.arange(0, axis_dim, 2, dtype=np.float32) / axis_dim)).astype(np.float32)
    t = np.arange(t0, t0 + TL)
    row = (t // GRID_W).astype(np.float32)
    col = (t % GRID_W).astype(np.float32)
    cos = np.zeros((128, TL), np.float32)
    sin = np.zeros((128, TL), np.float32)
    for p in range(128):
        d = p % 64
        pos = row if d < 32 else col
        ang = (pos * inv[d % 16]).astype(np.float32)
        cos[p] = np.cos(ang)
        sin[p] = np.sin(ang)
    return cos, sin


def const_mats():
    blk = np.zeros((128, 128), np.float32)
    blk[:64, :64] = 1
    blk[64:, 64:] = 1
    rr = np.zeros((128, 128), np.float32)
    for m in range(128):
        if m % 32 < 16:
            rr[m + 16, m] = -1
        else:
            rr[m - 16, m] = 1
    return blk, rr


def prep_common(inp):
    d = {}
    c = np.asarray(inp["c"], np.float32).reshape(D)
    cc = np.asarray(inp["c_ctx"], np.float32).reshape(D)
    d["cvec"] = np.ascontiguousarray(np.stack([vec_fm(c, KC), vec_fm(cc, KC)], axis=-1))
    mw = np.asarray(inp["mod_w"], np.float32)
    d["modw"] = np.ascontiguousarray(
        mw.reshape(2, KC, 128, 36, 256).transpose(0, 3, 2, 1, 4).reshape(2, 36, 128, KC * 256))
    mb = np.asarray(inp["mod_b"], np.float32)
    d["modb"] = np.ascontiguousarray(mb.reshape(2, 72, 128).transpose(0, 2, 1))
    ng = np.asarray(inp["norm_g"], np.float32)
    d["normg"] = np.ascontiguousarray(ng.reshape(2, 3, KC, 128).transpose(0, 3, 1, 2))
    w13 = np.asarray(inp["ffn_w13"], np.float32).reshape(4, KC, 128, 2, NJ, 128)
    d["w13"] = np.ascontiguousarray(w13.transpose(0, 4, 2, 1, 3, 5).reshape(4, NJ, 128, KC * 2 * 128))
    w2 = np.asarray(inp["ffn_w2"], np.float32).reshape(4, NJ, 128, KC, 128)
    d["w2"] = np.ascontiguousarray(w2.transpose(0, 3, 2, 1, 4).reshape(4, KC, 128, NJ * 128))
    d["onesb"] = np.ones((128, 128), np.float32)
    wi = np.asarray(inp["ab_w_in"], np.float32)[0]
    u, v, q, k, va = wi[:, 0:512], wi[:, 512:1024], wi[:, 1024:1536], wi[:, 1536:1664], wi[:, 1664:1792]
    kd = np.concatenate([k[:, 0:64], k[:, 0:64], k[:, 64:128], k[:, 64:128]], axis=1)
    d["winfm"] = wfm(np.concatenate([u, q, kd], axis=1))
    d["wintm"] = wfm(np.concatenate([v, va], axis=1))
    spw = np.asarray(inp["ab_spatial_w"], np.float32)[0]
    d["wsT"] = np.ascontiguousarray(spw.transpose(2, 0, 1).reshape(128, 8 * 128))
    d["bsrow"] = np.ascontiguousarray(np.asarray(inp["ab_spatial_b"], np.float32)[0].reshape(1, 8 * 128))
    d["gng_rep"] = np.ascontiguousarray(np.broadcast_to(np.asarray(inp["ab_gate_norm_g"], np.float32)[0][None, :], (128, 512)))
    gq = np.asarray(inp["ab_q_norm_g"], np.float32)[0]
    gk = np.asarray(inp["ab_k_norm_g"], np.float32)[0]
    d["gqk_pp"] = np.ascontiguousarray(np.stack([np.tile(gq, 2), np.tile(gk, 2)], axis=1))
    d["gqk_rep"] = np.ascontiguousarray(np.broadcast_to(np.stack([gq, gk])[None], (128, 2, 64)))
    blk, rr = const_mats()
    d["blk64"], d["rrot"] = blk, rr
    wo = np.asarray(inp["ab_w_out"], np.float32)[0]
    d["wo_g"] = np.ascontiguousarray(wo[0:512].reshape(4, 128, 1024).transpose(1, 0, 2).reshape(128, 4 * 1024))
    d["wo_a"] = np.ascontiguousarray(wo[512:].reshape(8, 64, 1024).transpose(1, 0, 2).reshape(64, 8 * 1024))
    return d


_PROGS = {}


def get_prog(TL, stage, NK):
    key = (TL, stage, NK)
    if key not in _PROGS:
        _PROGS[key] = build_program(TL, stage, NK)
    return _PROGS[key][0]


def pick(d, names):
    return {k: d[k] for k in names}


def prep_layer1(inp, d):
    wi = np.asarray(inp["ml_w_in"], np.float32)[0]
    q, k, v, o, g = wi[:, 0:512], wi[:, 512:1024], wi[:, 1024:2048], wi[:, 2048:3072], wi[:, 3072:3104]
    d["w1fm"] = wfm(np.concatenate([q, k, o], axis=1))
    d["w1tm"] = wfm(np.concatenate([v, g], axis=1))
    d["gateb_rep"] = np.ascontiguousarray(np.broadcast_to(np.asarray(inp["ml_gate_b"], np.float32)[0][None, :], (128, 32)))
    cw = np.asarray(inp["ml_conv_w"], np.float32)[0]
    d["convw"] = np.ascontiguousarray(cw.reshape(3, KC, 128).transpose(2, 1, 0))
    one = np.ones((128, 128), np.float32)
    d["tri_f"] = np.triu(one)
    d["tri_b"] = np.tril(one)
    d["ones_f"] = one
    d["ident_f"] = np.eye(128, dtype=np.float32)
    s = np.arange(128)[:, None]
    j = np.arange(128)[None, :]
    mf = np.where(s > j, -30000.0, 0.0).astype(np.float32)
    mb = np.where(s < j, -30000.0, 0.0).astype(np.float32)
    d["maskneg"] = np.ascontiguousarray(np.stack([mf, mb], axis=1).reshape(128, 256))
    wo = np.asarray(inp["ml_w_out"], np.float32)[0]
    d["wo1"] = np.ascontiguousarray(wo.reshape(8, 128, 1024).transpose(1, 0, 2).reshape(128, 8 * 1024))
    d["outg"] = vec_fm(np.asarray(inp["ml_out_norm_g"], np.float32)[0], 8)
    d["fng"] = vec_fm(np.asarray(inp["final_norm_g"], np.float32), 8)


A_IN = ["cvec", "modw", "modb", "normg", "w13", "w2", "onesb", "winfm", "wintm", "wsT", "bsrow", "gng_rep",
        "gqk_pp", "blk64", "rrot"]
B_IN = ["normg", "w13", "w2", "onesb", "gqk_rep", "wo_g", "wo_a", "w1fm", "w1tm", "gateb_rep"]
C_IN = ["normg", "onesb", "convw", "tri_f", "tri_b", "ones_f", "ident_f"]
D_IN = ["normg", "w13", "w2", "onesb", "ident_f", "maskneg", "wo1", "outg", "fng"]


def run_all(inputs, TL, ncores, debug=None):
    x = np.asarray(inputs["x"], np.float32)[0]
    ctx = np.asarray(inputs["ctx"], np.float32)[0]
    NK = TL * ncores
    ids = list(range(ncores))
    com = prep_common(inputs)
    prep_layer1(inputs, com)
    ctxT = fm(ctx)
    maps = []
    for i in ids:
        m = pick(com, A_IN)
        m["xT"] = np.concatenate([fm(x[i * TL:(i + 1) * TL]), ctxT], axis=2)
        m["cosT"], m["sinT"] = rope_tables(i * TL, TL)
        maps.append(m)
    rA = run_bass_kernel_spmd(get_prog(TL, "A", NK), maps, core_ids=ids).results
    kT_all = np.ascontiguousarray(np.concatenate([r["kT"][:, :, 0:TL] for r in rA] + [rA[0]["kT"][:, :, TL:]], axis=2))
    va_all = np.ascontiguousarray(np.concatenate([r["vaug"][:, 0:TL // 128] for r in rA] + [rA[0]["vaug"][:, TL // 128:]], axis=1))
    maps = []
    for i in ids:
        m = pick(com, B_IN)
        m.update(hT_in=rA[i]["hT_out"], modT_in=rA[i]["modT_out"], qT=rA[i]["qT"], gT=rA[i]["gT"],
                 kT_all=kT_all, vaug_all=va_all)
        maps.append(m)
    rB = run_bass_kernel_spmd(get_prog(TL, "B", NK), maps, core_ids=ids).results
    maps = []
    for i in ids:
        m = pick(com, C_IN)
        halo = np.zeros((128, KC, 2), np.float32)
        if i > 0:
            halo[:, :, 0] = rB[i - 1]["pqkT"][:, :, TL - 1]
        if i < ncores - 1:
            halo[:, :, 1] = rB[i + 1]["pqkT"][:, :, 0]
        m.update(pqkT=rB[i]["pqkT"], halo=halo, gates=rB[i]["gates"], vaug1=rB[i]["vaug1"])
        maps.append(m)
    rC = run_bass_kernel_spmd(get_prog(TL, "C", NK), maps, core_ids=ids).results
    maps = []
    for i in ids:
        m = pick(com, D_IN)
        seqA = np.ones((128, 2, 7, 4), np.float32)
        seqL = np.zeros((128, 2, 7, 4 * 129), np.float32)
        for slot, j in enumerate(range(0, i)):
            seqA[:, 0, slot] = rC[j]["Aagg"][:, 0]
            seqL[:, 0, slot] = rC[j]["Lagg"][:, 0].reshape(128, 516)
        for slot, j in enumerate(range(ncores - 1, i, -1)):
            seqA[:, 1, slot] = rC[j]["Aagg"][:, 1]
            seqL[:, 1, slot] = rC[j]["Lagg"][:, 1].reshape(128, 516)
        m.update(hT_in=rB[i]["hT_out"], modT_in=rA[i]["modT_out"], qkT=rC[i]["qkT"], vaug1=rB[i]["vaug1"],
                 sigoT=rB[i]["sigoT"], garr=rC[i]["garr"], seqA=seqA, seqL=seqL, Lctx=rC[0]["Lctx"])
        maps.append(m)
    rD = run_bass_kernel_spmd(get_prog(TL, "D", NK), maps, core_ids=ids).results
    if debug is not None:
        debug.update(rA=rA, rB=rB, rC=rC, rD=rD)
    out = np.concatenate([r["outT"].transpose(2, 1, 0).reshape(TL, D) for r in rD], axis=0)
    return out[None].astype(np.float32)


def kernel(**inputs):
    return run_all(inputs, SEQ // N_CORES, N_CORES)
```

```python
import numpy as np
import ml_dtypes
from contextlib import ExitStack
import concourse.bass as bass
import concourse.mybir as mybir
from concourse.bass_utils import run_bass_kernel_spmd

F32 = mybir.dt.float32
BF16 = mybir.dt.bfloat16
AF = mybir.ActivationFunctionType
ALU = mybir.AluOpType
AX = mybir.AxisListType

D = 1024
KC = 8
FH = 2816
NJ = 22
CTX = 256
EPS = 1e-6
N_CORES = 8
SEQ = 16384

ENG_ATTR = {"pe": "tensor", "act": "scalar", "dve": "vector", "pool": "gpsimd", "sp": "sync"}
SAME_ENGINE_SYNC = True
N_DMA_SEMS = 8


class Op:
    __slots__ = ("eng", "fn", "reads", "writes", "dma", "idx", "deps", "marked", "ev", "pre")

    def __init__(self, eng, fn, reads, writes, dma):
        self.eng, self.fn, self.reads, self.writes, self.dma = eng, fn, tuple(reads), tuple(writes), dma
        self.deps = set()
        self.marked = False
        self.ev = None
        self.pre = None


class Prog:
    def __init__(self, nc, es):
        self.nc, self.es = nc, es
        self.ops = []

    def add(self, eng, fn, reads=(), writes=(), dma=False):
        op = Op(eng, fn, reads, writes, dma)
        op.idx = len(self.ops)
        self.ops.append(op)
        return op

    def dma(self, q, out, in_, reads=(), writes=(), **kw):
        return self.add(q, lambda e: e.dma_start(out=out, in_=in_, **kw), reads, writes, dma=True)

    def mm(self, out, lhsT, rhs, start, stop, reads, writes):
        return self.add("pe", lambda e: e.matmul(out, lhsT, rhs, start=start, stop=stop), reads, writes)

    def fence(self):
        self.ops.append("FENCE")

    def emit(self):
        nc, es = self.nc, self.es
        raw = self.ops
        ops = []
        fence_before = set()
        for o in raw:
            if isinstance(o, str):
                fence_before.add(len(ops))
            else:
                o.idx = len(ops)
                ops.append(o)
        last_w = {}
        readers = {}
        last_compute = {}
        dma_slot_last = {}
        dcount0 = {}
        fence_deps = set()
        fence_pending = set()
        for op in ops:
            if op.idx in fence_before:
                fence_deps = set(last_compute.values()) | set(dma_slot_last.values())
                fence_pending = set(ENG_ATTR)
            deps = set()
            if op.eng in fence_pending:
                fence_pending.discard(op.eng)
                deps |= fence_deps
            if op.dma:
                k = dcount0.get(op.eng, 0)
                dcount0[op.eng] = k + 1
                dma_slot_last[(op.eng, k % N_DMA_SEMS)] = op.idx
            else:
                last_compute[op.eng] = op.idx
            for k in op.reads:
                if k in last_w:
                    deps.add(last_w[k])
            for k in op.writes:
                if k in last_w:
                    deps.add(last_w[k])
                for r in readers.get(k, ()):
                    deps.add(r)
            deps.discard(op.idx)
            for d in deps:
                p = ops[d]
                if not p.dma and not op.dma and p.eng == op.eng:
                    if p.eng == "pe" or not SAME_ENGINE_SYNC:
                        continue
                op.deps.add(d)
                p.marked = True
            for k in op.reads:
                readers.setdefault(k, []).append(op.idx)
            for k in op.writes:
                last_w[k] = op.idx
                readers[k] = []
        eng_sem = {e: es.enter_context(nc.semaphore("sem_" + e)) for e in ENG_ATTR}
        dma_sems = {e: [es.enter_context(nc.semaphore("dsem_%s_%d" % (e, i))) for i in range(N_DMA_SEMS)]
                    for e in ("sp", "pool", "act")}
        cnt = {e: 0 for e in ENG_ATTR}
        dcount = {e: 0 for e in dma_sems}
        duse = {e: [0] * N_DMA_SEMS for e in dma_sems}
        for op in ops:
            if op.dma:
                q = op.eng
                si = dcount[q] % N_DMA_SEMS
                dcount[q] += 1
                if duse[q][si] > 0:
                    op.pre = (dma_sems[q][si], 16 * duse[q][si])
                duse[q][si] += 1
                op.ev = (dma_sems[q][si], 16 * duse[q][si])
            elif op.marked:
                cnt[op.eng] += 1
                op.ev = (eng_sem[op.eng], cnt[op.eng])
        per_eng = {e: [] for e in ENG_ATTR}
        for op in ops:
            per_eng[op.eng].append(op)
        self.n_waits = 0

        def run_engine(ename, e):
            waited = {}
            for op in per_eng[ename]:
                need = {}
                if op.pre is not None:
                    need[id(op.pre[0])] = op.pre
                for d in op.deps:
                    sem, val = ops[d].ev
                    cur = need.get(id(sem))
                    if cur is None or cur[1] < val:
                        need[id(sem)] = (sem, val)
                for sid, (sem, val) in need.items():
                    if waited.get(sid, 0) >= val:
                        continue
                    e.wait_ge(sem, val)
                    waited[sid] = val
                    self.n_waits += 1
                ins = op.fn(e)
                if op.dma:
                    ins.then_inc(op.ev[0], 16)
                elif op.marked:
                    ins.then_inc(op.ev[0], 1)

        with nc.Block() as block:
            @block.tensor
            def _(e):
                run_engine("pe", e)

            @block.scalar
            def _(e):
                run_engine("act", e)

            @block.vector
            def _(e):
                run_engine("dve", e)

            @block.gpsimd
            def _(e):
                run_engine("pool", e)

            @block.sync
            def _(e):
                run_engine("sp", e)


class Arena:
    def __init__(self, t, n):
        self.t, self.n, self.off = t, n, 0

    def reset(self):
        self.off = 0

    def take(self, *shape):
        sz = int(np.prod(shape))
        ap = self.t[:, self.off:self.off + sz]
        self.off += sz
        assert self.off <= self.n, ("arena overflow", self.off, self.n)
        if len(shape) == 2:
            return ap.rearrange("p (a b) -> p a b", a=shape[0])
        if len(shape) == 3:
            return ap.rearrange("p (a b c) -> p a b c", a=shape[0], b=shape[1])
        return ap


def col_tiles(c0, n, w=512):
    out = []
    c = c0
    while c < c0 + n:
        ww = min(w, c0 + n - c)
        out.append((c, ww))
        c += ww
    return out


ABF_N = 53888
AF_N = 6912


def build_program(TL, stage, NK=None):
    NCOL = TL + CTX
    NB = NCOL // 128
    NKT = (NK or TL) + CTX
    NKB = NKT // 128
    nc = bass.Bass("TRN2", target_bir_lowering=False)
    es = ExitStack()
    P = Prog(nc, es)

    def din(name, shape, dt=F32):
        return nc.dram_tensor(name, list(shape), dt, kind="ExternalInput").ap()

    def dout(name, shape, dt=F32):
        return nc.dram_tensor(name, list(shape), dt, kind="ExternalOutput").ap()

    def sb(name, shape, dt):
        return es.enter_context(nc.sbuf_tensor(name, list(shape), dt))

    lat_tiles = [(c, w, 0) for (c, w) in col_tiles(0, TL)]
    ctx_tiles = [(c, w, 1) for (c, w) in col_tiles(TL, CTX)]
    all_tiles = lat_tiles + ctx_tiles
    hkeys = [("hT", kc, c0) for kc in range(KC) for (c0, w, s) in all_tiles]

    hT = sb("hT", [128, KC, NCOL], F32)
    abf_t = sb("abf", [128, ABF_N], BF16)
    af_t = sb("af", [128, AF_N], F32)
    ABF = Arena(abf_t, ABF_N)
    AFA = Arena(af_t, AF_N)
    onesb = sb("onesb_s", [128, 128], BF16)
    modT = sb("modT", [128, 2, 72, 2], F32)
    normg_s = sb("normg_s", [128, 2, 3, KC], F32)
    Avec = sb("Avec", [128, 2, 3, KC, 2], F32)
    Gvec = sb("Gvec", [128, 2, 3, KC, 2], F32)
    pbank = [es.enter_context(nc.psum_tensor("pb%d" % i, [128, 512], F32)) for i in range(7)]
    if stage in ("C", "D"):
        pbf = es.enter_context(nc.psum_tensor("pb7", [128, 1024], BF16))
    else:
        pbank.append(es.enter_context(nc.psum_tensor("pb7", [128, 512], F32)))
    outs_pending = []

    onesb_d = din("onesb", [128, 128])
    normg = din("normg", [2, 128, 3, KC])
    P.dma("pool", onesb[:], onesb_d[:, :], writes=["onesb"])
    P.dma("sp", normg_s[:], normg.rearrange("l p i k -> p l i k"), writes=["normg_s"])

    def compute_mod():
        cvec = din("cvec", [128, KC, 2])
        modw = din("modw", [2, 36, 128, KC * 256])
        modb = din("modb", [2, 128, 72])
        AFA.reset()
        cs = AFA.take(KC, 2)
        sT = AFA.take(KC, 2)
        modb_s = AFA.take(2, 72)
        modwb = [AFA.take(KC, 256) for _ in range(2)]
        P.dma("sp", cs, cvec[:, :, :], writes=["cs"])
        P.dma("sp", modb_s, modb.rearrange("l p c -> p l c"), writes=["modb_s"])
        P.add("act", lambda e: e.activation(sT, cs, AF.Silu), reads=["cs"], writes=["sT"])
        for l in range(2):
            mp = pbank[7]
            for pc in range(36):
                slot = (l * 36 + pc) % 2
                P.dma("sp", modwb[slot], modw[l, pc].rearrange("p (k f) -> p k f", k=KC),
                      writes=[("modwb", slot)])
                for fc in range(2):
                    ch = pc * 2 + fc
                    for kc in range(KC):
                        P.mm(mp[:, ch * 2:ch * 2 + 2], modwb[slot][:, kc, fc * 128:(fc + 1) * 128], sT[:, kc, :],
                             kc == 0, kc == KC - 1, reads=[("modwb", slot), "sT"], writes=[("pb", 7)])
            P.add("dve", lambda e, l=l, mp=mp: e.tensor_tensor(
                out=modT[:, l], in0=mp[:, 0:144].rearrange("p (c s) -> p c s", s=2),
                in1=modb_s[:, l].unsqueeze(2).to_broadcast([128, 72, 2]), op=ALU.add),
                reads=[("pb", 7), "modb_s"], writes=["modT"])

    def derive_mod():
        for l in range(2):
            for i in range(3):
                for s in range(2):
                    sc = modT[:, l, (3 * i + 1) * 8:(3 * i + 2) * 8, s]
                    gt = modT[:, l, (3 * i + 2) * 8:(3 * i + 3) * 8, s]
                    P.add("dve", lambda e, l=l, i=i, s=s, sc=sc: e.scalar_tensor_tensor(
                        out=Avec[:, l, i, :, s], in0=sc, scalar=1.0, in1=normg_s[:, l, i, :],
                        op0=ALU.add, op1=ALU.mult), reads=["modT", "normg_s"], writes=["Avec"])
                    P.add("dve", lambda e, l=l, i=i, s=s, gt=gt: e.tensor_scalar(
                        out=Gvec[:, l, i, :, s], in0=gt, scalar1=(1.0 if i == 1 else 0.5), scalar2=None,
                        op0=ALU.mult), reads=["modT"], writes=["Gvec"])

    def Bvec(l, i, kc, s):
        return modT[:, l, 3 * i * 8 + kc, s:s + 1]

    state = {"tmp": 0, "sg": 0, "pb_g": 0, "w13": 0, "w2": 0}

    def modulate(l, i, tiles, zT, zbase, sq, rstd, tmpA):
        for (c0, w, s) in tiles:
            ssb = pbank[6]
            for kc in range(KC):
                P.add("act", lambda e, kc=kc, c0=c0, w=w: e.activation(sq[:, kc, 0:w], hT[:, kc, c0:c0 + w], AF.Square),
                      reads=[("hT", kc, c0)], writes=[("sq", kc)])
            for kc in range(KC):
                P.mm(ssb[:, 0:w], onesb[:], sq[:, kc, 0:w], kc == 0, kc == KC - 1,
                     reads=["onesb", ("sq", kc)], writes=[("pb", 6)])
            P.add("act", lambda e, w=w, ssb=ssb: e.activation(rstd[:, 0:w], ssb[:, 0:w], AF.Sqrt, bias=EPS, scale=1.0 / D),
                  reads=[("pb", 6)], writes=["rstd"])
            P.add("dve", lambda e, w=w: e.reciprocal(rstd[:, 0:w], rstd[:, 0:w]), reads=["rstd"], writes=["rstd"])
            for kc in range(KC):
                ti = state["tmp"] % 2
                state["tmp"] += 1
                P.add("dve", lambda e, kc=kc, c0=c0, w=w, s=s, ti=ti: e.scalar_tensor_tensor(
                    out=tmpA[ti][:, 0:w], in0=hT[:, kc, c0:c0 + w], scalar=Avec[:, l, i, kc, s:s + 1],
                    in1=rstd[:, 0:w], op0=ALU.mult, op1=ALU.mult),
                    reads=[("hT", kc, c0), "Avec", "rstd"], writes=[("tmpA", ti)])
                P.add("act", lambda e, kc=kc, c0=c0, w=w, s=s, ti=ti: e.activation(
                    zT[:, kc, c0 - zbase:c0 - zbase + w], tmpA[ti][:, 0:w], AF.Identity,
                    bias=Bvec(l, i, kc, s), scale=1.0),
                    reads=[("tmpA", ti), "modT"], writes=[("zT", kc, c0 - zbase)])

    ffn_ids = {"A": [0], "B": [1, 2], "D": [3]}.get(stage, [])
    w13 = din("w13", [len(ffn_ids), NJ, 128, KC * 2 * 128]) if ffn_ids else None
    w2 = din("w2", [len(ffn_ids), KC, 128, NJ * 128]) if ffn_ids else None

    def ffn(l, f, with_ctx=True):
        P.fence()
        ABF.reset()
        AFA.reset()
        fi = ffn_ids.index(l * 2 + f)
        ni = 0 if f == 0 else 2
        GMAX = 1280 if TL >= 1024 else NCOL
        zT = ABF.take(KC, GMAX)
        hid = ABF.take(NJ, GMAX)
        sq = ABF.take(KC, 512)
        w13b = [ABF.take(KC, 2, 128) for _ in range(2)]
        w2b = [ABF.take(NJ, 128) for _ in range(2)]
        rstd = AFA.take(512)
        tmpA = [AFA.take(512) for _ in range(2)]
        sg = [AFA.take(512) for _ in range(2)]
        ctxt = ctx_tiles if with_ctx else []
        if TL >= 1024:
            half = len(lat_tiles) // 2
            groups = [lat_tiles[:half], lat_tiles[half:] + ctxt]
        else:
            groups = [lat_tiles + ctxt]
        for grp in groups:
            zbase = grp[0][0]
            modulate(l, ni, grp, zT, zbase, sq, rstd, tmpA)
            for j in range(NJ):
                slot = state["w13"] % 2
                state["w13"] += 1
                P.dma("pool", w13b[slot], w13[fi, j].rearrange("p (k g c) -> p k g c", k=KC, g=2),
                      writes=[("w13b", slot)])
                for (c0, w, s) in grp:
                    zc = c0 - zbase
                    b = state["pb_g"] % 2
                    state["pb_g"] += 1
                    gp, up = pbank[b], pbank[2 + b]
                    for kc in range(KC):
                        P.mm(gp[:, 0:w], w13b[slot][:, kc, 0, :], zT[:, kc, zc:zc + w], kc == 0, kc == KC - 1,
                             reads=[("w13b", slot), ("zT", kc, zc)], writes=[("pb", b)])
                    for kc in range(KC):
                        P.mm(up[:, 0:w], w13b[slot][:, kc, 1, :], zT[:, kc, zc:zc + w], kc == 0, kc == KC - 1,
                             reads=[("w13b", slot), ("zT", kc, zc)], writes=[("pb", 2 + b)])
                    si = state["sg"] % 2
                    state["sg"] += 1
                    P.add("act", lambda e, w=w, gp=gp, si=si: e.activation(sg[si][:, 0:w], gp[:, 0:w], AF.Silu),
                          reads=[("pb", b)], writes=[("sg", si)])
                    P.add("dve", lambda e, w=w, up=up, si=si, j=j, zc=zc: e.tensor_tensor(
                        out=hid[:, j, zc:zc + w], in0=sg[si][:, 0:w], in1=up[:, 0:w], op=ALU.mult),
                        reads=[("sg", si), ("pb", 2 + b)], writes=[("hid", j, zc)])
            for dc in range(KC):
                slot = state["w2"] % 2
                state["w2"] += 1
                P.dma("pool", w2b[slot], w2[fi, dc].rearrange("p (j c) -> p j c", j=NJ),
                      writes=[("w2b", slot)])
                for (c0, w, s) in grp:
                    zc = c0 - zbase
                    b = 4 + state["pb_g"] % 2
                    state["pb_g"] += 1
                    op_ = pbank[b]
                    for j in range(NJ):
                        P.mm(op_[:, 0:w], w2b[slot][:, j, :], hid[:, j, zc:zc + w], j == 0, j == NJ - 1,
                             reads=[("w2b", slot), ("hid", j, zc)], writes=[("pb", b)])
                    P.add("dve", lambda e, w=w, op_=op_, dc=dc, c0=c0, s=s: e.scalar_tensor_tensor(
                        out=hT[:, dc, c0:c0 + w], in0=op_[:, 0:w], scalar=Gvec[:, l, ni, dc, s:s + 1],
                        in1=hT[:, dc, c0:c0 + w], op0=ALU.mult, op1=ALU.add),
                        reads=[("pb", b), "Gvec", ("hT", dc, c0)], writes=[("hT", dc, c0)])

    def store(q, dram_ap, sb_ap, reads):
        op = P.dma(q, dram_ap, sb_ap, reads=reads, writes=[("out", len(outs_pending))])
        outs_pending.append(("out", len(outs_pending)))
        return op

    def finish():
        P.add("sp", lambda e: e.nop(), reads=list(outs_pending), writes=["done"])
        P.emit()

    def mixer0_project():
        winfm_d = din("winfm", [128, KC * 1280])
        wintm_d = din("wintm", [128, KC * 640])
        wsT_d = din("wsT", [128, 8 * 128])
        bsrow_d = din("bsrow", [1, 8 * 128])
        gng_d = din("gng_rep", [128, 512])
        gqk_d = din("gqk_pp", [128, 2])
        blk64_d = din("blk64", [128, 128])
        rrot_d = din("rrot", [128, 128])
        cos_d = din("cosT", [128, TL])
        sin_d = din("sinT", [128, TL])
        qT_o = dout("qT", [128, 4, NCOL], BF16)
        kT_o = dout("kT", [128, 2, NCOL], BF16)
        va_o = dout("vaug", [128, NB, 2, 65], BF16)
        gT_o = dout("gT", [128, 4, NCOL], BF16)
        P.fence()
        ABF.reset()
        AFA.reset()
        winfm = ABF.take(KC, 1280)
        wintm = ABF.take(KC, 640)
        wsT = ABF.take(8, 128)
        blk64 = ABF.take(128)
        rrot = ABF.take(128)
        zT = ABF.take(KC, 512)
        sq = ABF.take(KC, 512)
        uT = ABF.take(4, 512)
        sqh = ABF.take(512)
        qnb = ABF.take(512)
        vnb = ABF.take(512)
        qkT = ABF.take(6, NCOL)
        gT = ABF.take(4, NCOL)
        vaug = ABF.take(NB, 2, 65)
        rstd = AFA.take(512)
        tmpA = [AFA.take(512) for _ in range(2)]
        rs = AFA.take(512)
        qn = AFA.take(512)
        t1 = AFA.take(512)
        vg = AFA.take(512)
        sqv = AFA.take(512)
        gng = AFA.take(512)
        cosb = AFA.take(512)
        sinb = AFA.take(512)
        ssg = AFA.take(8)
        gqk = AFA.take(2)
        onesrow = AFA.take(64)
        bsrow = AFA.take(8, 128)
        P.dma("pool", winfm, winfm_d.rearrange("p (k f) -> p k f", k=KC), writes=["winfm"])
        P.dma("pool", wintm, wintm_d.rearrange("p (k f) -> p k f", k=KC), writes=["wintm"])
        P.dma("pool", wsT, wsT_d.rearrange("p (g t) -> p g t", g=8), writes=["wsT"])
        P.dma("pool", blk64, blk64_d[:, :], writes=["blk64"])
        P.dma("pool", rrot, rrot_d[:, :], writes=["rrot"])
        P.dma("sp", gng, gng_d[:, :], writes=["gng"])
        P.dma("sp", gqk, gqk_d[:, :], writes=["gqk"])
        P.dma("sp", bsrow[0:1], bsrow_d.rearrange("o (g t) -> o g t", g=8), writes=["bsrow"])
        P.add("dve", lambda e: e.memset(onesrow[0:1, :], 1.0), writes=["onesrow"])
        P.add("pool", lambda e: e.memset(vaug[:, :, :, 64:65], 1.0), writes=["vaug"])
        for (c0, w, s) in all_tiles:
            modulate(0, 1, [(c0, w, s)], zT, c0, sq, rstd, tmpA)
            zr = [("zT", kc, 0) for kc in range(KC)]
            if s == 0:
                P.dma("sp", cosb[:, 0:w], cos_d[:, c0:c0 + w], writes=["cosb"])
                P.dma("sp", sinb[:, 0:w], sin_d[:, c0:c0 + w], writes=["sinb"])
            for oc in range(10):
                b = oc % 2
                pp = pbank[b]
                for kc in range(KC):
                    P.mm(pp[:, 0:w], winfm[:, kc, oc * 128:(oc + 1) * 128], zT[:, kc, 0:w], kc == 0, kc == KC - 1,
                         reads=["winfm", ("zT", kc, 0)], writes=[("pb", b)])
                if oc < 4:
                    P.add("act", lambda e, pp=pp, oc=oc, w=w: e.activation(uT[:, oc, 0:w], pp[:, 0:w], AF.Gelu_apprx_tanh),
                          reads=[("pb", b)], writes=[("uT", oc)])
                    continue
                gi = 0 if oc < 8 else 1
                P.add("act", lambda e, pp=pp, w=w: e.activation(sqh[:, 0:w], pp[:, 0:w], AF.Square),
                      reads=[("pb", b)], writes=["sqh"])
                hp = pbank[2]
                P.mm(hp[:, 0:w], blk64, sqh[:, 0:w], True, True, reads=["blk64", "sqh"], writes=[("pb", 2)])
                P.add("act", lambda e, hp=hp, w=w: e.activation(rs[:, 0:w], hp[:, 0:w], AF.Sqrt, bias=EPS, scale=1.0 / 64),
                      reads=[("pb", 2)], writes=["rs"])
                P.add("dve", lambda e, w=w: e.reciprocal(rs[:, 0:w], rs[:, 0:w]), reads=["rs"], writes=["rs"])
                dst = qkT[:, oc - 4, c0:c0 + w]
                if s == 1:
                    P.add("dve", lambda e, pp=pp, w=w, gi=gi, dst=dst: e.scalar_tensor_tensor(
                        out=dst, in0=pp[:, 0:w], scalar=gqk[:, gi:gi + 1], in1=rs[:, 0:w], op0=ALU.mult, op1=ALU.mult),
                        reads=[("pb", b), "gqk", "rs"], writes=[("qkT", oc, c0)])
                    continue
                P.add("dve", lambda e, pp=pp, w=w, gi=gi: e.scalar_tensor_tensor(
                    out=qn[:, 0:w], in0=pp[:, 0:w], scalar=gqk[:, gi:gi + 1], in1=rs[:, 0:w], op0=ALU.mult, op1=ALU.mult),
                    reads=[("pb", b), "gqk", "rs"], writes=["qn"])
                P.add("act", lambda e, w=w: e.copy(qnb[:, 0:w], qn[:, 0:w]), reads=["qn"], writes=["qnb"])
                rp = pbank[3]
                P.mm(rp[:, 0:w], rrot, qnb[:, 0:w], True, True, reads=["rrot", "qnb"], writes=[("pb", 3)])
                P.add("pool", lambda e, w=w: e.tensor_tensor(out=t1[:, 0:w], in0=qn[:, 0:w], in1=cosb[:, 0:w], op=ALU.mult),
                      reads=["qn", "cosb"], writes=["t1"])
                P.add("dve", lambda e, rp=rp, w=w: e.tensor_tensor(out=qn[:, 0:w], in0=rp[:, 0:w], in1=sinb[:, 0:w], op=ALU.mult),
                      reads=[("pb", 3), "sinb", "qn"], writes=["qn"])
                P.add("dve", lambda e, w=w, dst=dst: e.tensor_tensor(out=dst, in0=qn[:, 0:w], in1=t1[:, 0:w], op=ALU.add),
                      reads=["qn", "t1"], writes=[("qkT", oc, c0)])
            for tb in range(w // 128):
                blk = (c0 + tb * 128) // 128
                vp = pbank[4 + (blk % 2)]
                vb = 4 + (blk % 2)
                for kc in range(KC):
                    P.mm(vp[:, 0:512], zT[:, kc, tb * 128:(tb + 1) * 128], wintm[:, kc, 0:512], kc == 0, kc == KC - 1,
                         reads=["wintm", ("zT", kc, 0)], writes=[("pb", vb)])
                vap = pbank[6]
                for kc in range(KC):
                    P.mm(vap[:, 0:128], zT[:, kc, tb * 128:(tb + 1) * 128], wintm[:, kc, 512:640], kc == 0, kc == KC - 1,
                         reads=["wintm", ("zT", kc, 0)], writes=[("pb", 6)])
                P.add("act", lambda e, vap=vap, blk=blk: e.copy(
                    vaug[:, blk, :, 0:64], vap[:, 0:128].rearrange("p (g d) -> p g d", g=2)),
                    reads=[("pb", 6)], writes=[("vaug", blk)])
                P.add("act", lambda e, vp=vp: e.activation(vg[:, :], vp[:, :], AF.Gelu_apprx_tanh),
                      reads=[("pb", vb)], writes=["vg"])
                P.add("pool", lambda e: e.tensor_tensor(out=sqv[:, :], in0=vg[:, :], in1=vg[:, :], op=ALU.mult),
                      reads=["vg"], writes=["sqv"])
                P.add("dve", lambda e: e.tensor_reduce(out=ssg[:, :], in_=sqv.rearrange("p (g d) -> p g d", g=8),
                                                       axis=AX.X, op=ALU.add), reads=["sqv"], writes=["ssg"])
                P.add("act", lambda e: e.activation(ssg[:, :], ssg[:, :], AF.Sqrt, bias=EPS, scale=1.0 / 64),
                      reads=["ssg"], writes=["ssg"])
                P.add("dve", lambda e: e.reciprocal(ssg[:, :], ssg[:, :]), reads=["ssg"], writes=["ssg"])
                P.add("dve", lambda e: e.tensor_tensor(
                    out=sqv.rearrange("p (g d) -> p g d", g=8), in0=vg.rearrange("p (g d) -> p g d", g=8),
                    in1=ssg.unsqueeze(2).to_broadcast([128, 8, 64]), op=ALU.mult),
                    reads=["vg", "ssg", "sqv"], writes=["sqv"])
                P.add("pool", lambda e: e.tensor_tensor(out=vnb[:, :], in0=sqv[:, :], in1=gng[:, :], op=ALU.mult),
                      reads=["sqv", "gng"], writes=["vnb"])
                sp_ = pbank[7]
                for c in range(4):
                    for gg in range(2):
                        g = 2 * c + gg
                        o_ = sp_[gg * 64:(gg + 1) * 64, c * 128:(c + 1) * 128]
                        P.mm(o_, vnb[:, g * 64:(g + 1) * 64], wsT[:, g, :], True, False,
                             reads=["vnb", "wsT"], writes=[("pb", 7)])
                        P.mm(o_, onesrow[0:1, 0:64], bsrow[0:1, g, :], False, True,
                             reads=["onesrow", "bsrow"], writes=[("pb", 7)])
                cb = c0 + tb * 128
                P.add("dve", lambda e, sp_=sp_, tb=tb, cb=cb: e.tensor_tensor(
                    out=gT[:, :, cb:cb + 128], in0=uT[:, :, tb * 128:(tb + 1) * 128],
                    in1=sp_[:, :].rearrange("p (c t) -> p c t", c=4), op=ALU.mult),
                    reads=[("pb", 7)] + [("uT", oc) for oc in range(4)], writes=[("gT", c0)])
        allq = [("qkT", oc, c0) for oc in range(4, 10) for (c0, w, s) in all_tiles]
        store("sp", qT_o[:, :, :], qkT[:, 0:4, :], allq)
        store("sp", kT_o[:, :, :], qkT[:, 4:6, :], allq)
        store("sp", gT_o[:, :, :], gT, [("gT", c0) for (c0, w, s) in all_tiles])
        store("sp", va_o[:, :, :, :], vaug, ["vaug"] + [("vaug", b) for b in range(NB)])

    def mixer0_attend():
        qT_d = din("qT", [128, 4, NCOL], BF16)
        gT_d = din("gT", [128, 4, NCOL], BF16)
        kall_d = din("kT_all", [128, 2, NKT], BF16)
        vall_d = din("vaug_all", [128, NKB, 2, 65], BF16)
        gqkrep_d = din("gqk_rep", [128, 2, 64])
        wog_d = din("wo_g", [128, 4 * 1024])
        woa_d = din("wo_a", [64, 8 * 1024])
        P.fence()
        ABF.reset()
        AFA.reset()
        qT = ABF.take(4, NCOL)
        attnT = ABF.take(8, NCOL)
        KSB = 8
        kbuf = [ABF.take(KSB * 128) for _ in range(2)]
        vbuf = [ABF.take(KSB, 65) for _ in range(2)]
        pT = [ABF.take(512) for _ in range(3)]
        gqkrep = AFA.take(2, 64)
        mx = AFA.take(2)
        negB = AFA.take(1)
        accs = AFA.take(512)
        rec = AFA.take(512)
        onesf = AFA.take(64)
        P.dma("sp", qT, qT_d[:, :, :], writes=["qT"])
        P.dma("sp", gqkrep, gqkrep_d[:, :, :], writes=["gqkrep"])
        P.add("dve", lambda e: e.memset(onesf[:, :], 1.0), writes=["onesf"])
        P.add("dve", lambda e: e.tensor_reduce(out=mx[:, :], in_=gqkrep, axis=AX.X, op=ALU.max, apply_absolute_value=True),
              reads=["gqkrep"], writes=["mx"])
        P.add("dve", lambda e: e.scalar_tensor_tensor(out=negB[:, :], in0=mx[:, 0:1], scalar=-8.0, in1=mx[:, 1:2],
                                                      op0=ALU.mult, op1=ALU.mult), reads=["mx"], writes=["negB"])
        st = {"kv": 0, "pt": 0, "sc": 0}
        for (c0, w, s) in all_tiles:
            if s == 0:
                kblocks = list(range(NKB))
            else:
                kblocks = list(range(NKB - CTX // 128, NKB))
            sbs = [kblocks[i:i + KSB] for i in range(0, len(kblocks), KSB)]
            for g in range(2):
                for si, sblk in enumerate(sbs):
                    slot = st["kv"] % 2
                    st["kv"] += 1
                    nb = len(sblk)
                    k0 = sblk[0] * 128
                    P.dma("sp", kbuf[slot][:, 0:nb * 128], kall_d[:, g, k0:k0 + nb * 128], writes=[("kbuf", slot)])
                    P.dma("sp", vbuf[slot][:, 0:nb, :], vall_d[:, sblk[0]:sblk[0] + nb, g, :], writes=[("vbuf", slot)])
                    for bi, kb in enumerate(sblk):
                        first = (si == 0 and bi == 0)
                        last = (si == len(sbs) - 1 and bi == nb - 1)
                        for hh in range(4):
                            h = 4 * g + hh
                            off = 64 * (h % 2)
                            scb = 4 + st["sc"] % 4
                            st["sc"] += 1
                            sp_ = pbank[scb]
                            P.mm(sp_[:, 0:w], kbuf[slot][off:off + 64, bi * 128:(bi + 1) * 128],
                                 qT[off:off + 64, h // 2, c0:c0 + w], True, True,
                                 reads=[("kbuf", slot), "qT"], writes=[("pb", scb)])
                            pi = st["pt"] % 3
                            st["pt"] += 1
                            P.add("act", lambda e, sp_=sp_, w=w, pi=pi: e.activation(
                                pT[pi][:, 0:w], sp_[:, 0:w], AF.Exp, bias=negB[:, 0:1], scale=0.125),
                                reads=[("pb", scb), "negB"], writes=[("pT", pi)])
                            P.mm(pbank[hh][0:65, 0:w], vbuf[slot][:, bi, :], pT[pi][:, 0:w], first, last,
                                 reads=[("vbuf", slot), ("pT", pi)], writes=[("pb", hh)])
                for hh in range(4):
                    h = 4 * g + hh
                    P.add("act", lambda e, hh=hh, w=w: e.copy(accs[0:65, 0:w], pbank[hh][0:65, 0:w]),
                          reads=[("pb", hh)], writes=["accs"])
                    P.add("dve", lambda e, w=w: e.reciprocal(rec[64:65, 0:w], accs[64:65, 0:w]),
                          reads=["accs"], writes=["rec"])
                    bp = pbank[4 + st["sc"] % 4]
                    bpk = ("pb", 4 + st["sc"] % 4)
                    st["sc"] += 1
                    P.mm(bp[0:64, 0:w], onesf[64:65, 0:64], rec[64:65, 0:w], True, True,
                         reads=["onesf", "rec"], writes=[bpk])
                    P.add("dve", lambda e, bp=bp, w=w, h=h, c0=c0: e.tensor_tensor(
                        out=attnT[0:64, h, c0:c0 + w], in0=accs[0:64, 0:w], in1=bp[0:64, 0:w], op=ALU.mult),
                        reads=["accs", bpk], writes=[("attnT", h, c0)])
        P.fence()
        gT = ABF.take(4, NCOL)
        wog = ABF.take(4, 1024)
        woa = ABF.take(8, 1024)
        P.dma("sp", gT, gT_d[:, :, :], writes=["gT"])
        P.dma("pool", wog, wog_d.rearrange("p (c f) -> p c f", c=4), writes=["wog"])
        P.dma("pool", woa[0:64], woa_d.rearrange("p (c f) -> p c f", c=8), writes=["woa"])
        for (c0, w, s) in all_tiles:
            for oc in range(KC):
                b = oc % 2
                yp = pbank[b]
                for c in range(4):
                    P.mm(yp[:, 0:w], wog[:, c, oc * 128:(oc + 1) * 128], gT[:, c, c0:c0 + w], c == 0, False,
                         reads=["wog", "gT"], writes=[("pb", b)])
                for h in range(8):
                    P.mm(yp[:, 0:w], woa[0:64, h, oc * 128:(oc + 1) * 128], attnT[0:64, h, c0:c0 + w], False, h == 7,
                         reads=["woa", ("attnT", h, c0)], writes=[("pb", b)])
                P.add("dve", lambda e, yp=yp, w=w, oc=oc, c0=c0, s=s: e.scalar_tensor_tensor(
                    out=hT[:, oc, c0:c0 + w], in0=yp[:, 0:w], scalar=Gvec[:, 0, 1, oc, s:s + 1],
                    in1=hT[:, oc, c0:c0 + w], op0=ALU.mult, op1=ALU.add),
                    reads=[("pb", b), "Gvec", ("hT", oc, c0)], writes=[("hT", oc, c0)])


    def mixer1_project():
        w1fm_d = din("w1fm", [128, KC * 2048])
        w1tm_d = din("w1tm", [128, KC * 1056])
        gb_d = din("gateb_rep", [128, 32])
        pqk_o = dout("pqkT", [128, KC, NCOL])
        sig_o = dout("sigoT", [128, KC, NCOL], BF16)
        va_o = dout("vaug1", [128, NB, 8, 129], BF16)
        gt_o = dout("gates", [128, NB, 32])
        P.fence()
        ABF.reset()
        AFA.reset()
        w1fm = ABF.take(KC, 2048)
        w1tm = ABF.take(KC, 1056)
        zT = ABF.take(KC, 512)
        sq = ABF.take(KC, 512)
        sigo = [ABF.take(KC, 512) for _ in range(2)]
        vaug = [ABF.take(4, 8, 129) for _ in range(2)]
        rstd = AFA.take(512)
        tmpA = [AFA.take(512) for _ in range(2)]
        pq = [AFA.take(512) for _ in range(4)]
        gb = AFA.take(32)
        gts = [AFA.take(4, 32) for _ in range(2)]
        P.dma("pool", w1fm, w1fm_d.rearrange("p (k f) -> p k f", k=KC), writes=["w1fm"])
        P.dma("pool", w1tm, w1tm_d.rearrange("p (k f) -> p k f", k=KC), writes=["w1tm"])
        P.dma("sp", gb, gb_d[:, :], writes=["gb"])
        for i in range(2):
            P.add("pool", lambda e, i=i: e.memset(vaug[i][:, :, :, 128:129], 1.0), writes=[("vaug1", i)])
        pqi = 0
        for ti, (c0, w, s) in enumerate(all_tiles):
            modulate(1, 1, [(c0, w, s)], zT, c0, sq, rstd, tmpA)
            tsl = ti % 2
            for oc in range(16):
                b = oc % 2
                pp = pbank[b]
                for kc in range(KC):
                    P.mm(pp[:, 0:w], w1fm[:, kc, oc * 128:(oc + 1) * 128], zT[:, kc, 0:w], kc == 0, kc == KC - 1,
                         reads=["w1fm", ("zT", kc, 0)], writes=[("pb", b)])
                if oc < 8:
                    pi = pqi % 4
                    pqi += 1
                    P.add("act", lambda e, pp=pp, w=w, pi=pi: e.copy(pq[pi][:, 0:w], pp[:, 0:w]),
                          reads=[("pb", b)], writes=[("pq", pi)])
                    P.dma("sp", pqk_o[:, oc, c0:c0 + w], pq[pi][:, 0:w], reads=[("pq", pi)], writes=[("pqo", pi)])
                    outs_pending.append(("pqo", pi))
                else:
                    P.add("act", lambda e, pp=pp, w=w, oc=oc, tsl=tsl: e.activation(
                        sigo[tsl][:, oc - 8, 0:w], pp[:, 0:w], AF.Sigmoid), reads=[("pb", b)], writes=[("sigo", tsl)])
            P.dma("sp", sig_o[:, :, c0:c0 + w], sigo[tsl][:, :, 0:w], reads=[("sigo", tsl)], writes=[("sigo_o", tsl)])
            outs_pending.append(("sigo_o", tsl))
            nbk = w // 128
            for tb in range(nbk):
                for half in range(2):
                    vb = 2 + half
                    vp = pbank[vb]
                    for kc in range(KC):
                        P.mm(vp[:, 0:512], zT[:, kc, tb * 128:(tb + 1) * 128], w1tm[:, kc, half * 512:(half + 1) * 512],
                             kc == 0, kc == KC - 1, reads=["w1tm", ("zT", kc, 0)], writes=[("pb", vb)])
                    P.add("act", lambda e, vp=vp, tb=tb, half=half, tsl=tsl: e.copy(
                        vaug[tsl][:, tb, half * 4:(half + 1) * 4, 0:128], vp[:, :].rearrange("p (h d) -> p h d", h=4)),
                        reads=[("pb", vb)], writes=[("vaug1", tsl)])
                gp_ = pbank[4]
                for kc in range(KC):
                    P.mm(gp_[:, 0:32], zT[:, kc, tb * 128:(tb + 1) * 128], w1tm[:, kc, 1024:1056], kc == 0, kc == KC - 1,
                         reads=["w1tm", ("zT", kc, 0)], writes=[("pb", 4)])
                P.add("dve", lambda e, gp_=gp_, tb=tb, tsl=tsl: e.tensor_tensor(
                    out=gts[tsl][:, tb, :], in0=gp_[:, 0:32], in1=gb[:, :], op=ALU.add),
                    reads=[("pb", 4), "gb"], writes=[("gts", tsl)])
            b0 = c0 // 128
            P.dma("sp", va_o[:, b0:b0 + nbk], vaug[tsl][:, 0:nbk], reads=[("vaug1", tsl)], writes=[("va_o", tsl)])
            outs_pending.append(("va_o", tsl))
            P.dma("sp", gt_o[:, b0:b0 + nbk, :], gts[tsl][:, 0:nbk, :], reads=[("gts", tsl)], writes=[("gt_o", tsl)])
            outs_pending.append(("gt_o", tsl))

    NBL = TL // 128

    def state_scan(S, blocks_by_dir, k_src, va_d, garr, ident_b, store_to, tag):
        kc_b = [ABF.take(4, 128) for _ in range(2)]
        va_b = [ABF.take(8, 129) for _ in range(2)]
        kw = [ABF.take(8, 64) for _ in range(2)]
        order = []
        nb = len(blocks_by_dir[0])
        for i in range(nb):
            order.append((0, blocks_by_dir[0][i]))
            order.append((1, blocks_by_dir[1][i]))
        li = 0
        for (d, blk) in order:
            sl = li % 2
            li += 1
            P.dma("sp", va_b[sl], va_d[:, blk], writes=[(tag + "va", sl)])
            if k_src[0] == "dram":
                P.dma("sp", kc_b[sl], k_src[1][:, 4:8, blk * 128:(blk + 1) * 128], writes=[(tag + "kc", sl)])
                for pr in range(4):
                    P.add("pe", lambda e, pr=pr, sl=sl: e.transpose(pbf[:, pr * 128:(pr + 1) * 128], kc_b[sl][:, pr, :], ident_b),
                          reads=[(tag + "kc", sl), "ident_b"], writes=["pbf"])
            else:
                for pr in range(4):
                    P.add("pe", lambda e, pr=pr, blk=blk: e.transpose(
                        pbf[:, pr * 128:(pr + 1) * 128], k_src[1][:, 4 + pr, blk * 128:(blk + 1) * 128], ident_b),
                        reads=[("qk_s", 4 + pr), "ident_b"], writes=["pbf"])
            P.add("dve", lambda e, sl=sl, d=d, blk=blk: e.tensor_tensor(
                out=kw[sl], in0=pbf[:, 0:512].rearrange("p (h k) -> p h k", h=8),
                in1=garr[:, blk, 16 + d * 8:16 + (d + 1) * 8].unsqueeze(2).to_broadcast([128, 8, 64]), op=ALU.mult),
                reads=["pbf", "garr"], writes=[(tag + "kw", sl)])
            for h in range(8):
                off = 64 * (h % 2)
                pr = h // 2
                bk = 4 + pr // 2
                P.mm(pbank[bk][off:off + 64, (pr % 2) * 129:(pr % 2) * 129 + 129], kw[sl][:, h, :], va_b[sl][:, h, :],
                     True, True, reads=[(tag + "kw", sl), (tag + "va", sl)], writes=[("pb", bk)])
            if store_to is not None:
                P.add("act", lambda e, d=d, blk=blk: e.copy(store_to[d][blk], S[:, d]),
                      reads=[("S", d)], writes=[("Sin", d, blk)])
            P.add("dve", lambda e, d=d, blk=blk: e.tensor_tensor(
                out=S[:, d], in0=S[:, d], in1=garr[:, blk, 48 + d * 4:48 + (d + 1) * 4].unsqueeze(2).to_broadcast([128, 4, 129]),
                op=ALU.mult), reads=[("S", d), "garr"], writes=[("S", d)])
            for hf in range(2):
                P.add("dve", lambda e, d=d, hf=hf: e.tensor_tensor(
                    out=S[:, d, 2 * hf:2 * hf + 2, :], in0=S[:, d, 2 * hf:2 * hf + 2, :],
                    in1=pbank[4 + hf][:, 0:258].rearrange("p (a b) -> p a b", a=2), op=ALU.add),
                    reads=[("S", d), ("pb", 4 + hf)], writes=[("S", d)])

    def stage_c():
        pqk_d = din("pqkT", [128, KC, NCOL])
        halo_d = din("halo", [128, KC, 2])
        convw_d = din("convw", [128, KC, 3])
        gates_d = din("gates", [128, NB, 32])
        va_d = din("vaug1", [128, NB, 8, 129], BF16)
        trif_d = din("tri_f", [128, 128])
        trib_d = din("tri_b", [128, 128])
        onesf_d = din("ones_f", [128, 128])
        identf_d = din("ident_f", [128, 128])
        qk_o = dout("qkT", [128, KC, NCOL], BF16)
        garr_o = dout("garr", [128, NB, 72])
        Lagg_o = dout("Lagg", [128, 2, 4, 129])
        Aagg_o = dout("Aagg", [128, 2, 4])
        Lctx_o = dout("Lctx", [128, 2, 4, 129])
        ABF.reset()
        AFA.reset()
        ident_b = ABF.take(128)
        qk_s = ABF.take(KC, NCOL)
        ext = [AFA.take(TL + 2)] * 2
        extc = AFA.take(CTX + 2)
        cv = AFA.take(512)
        convw = AFA.take(KC, 3)
        halo = AFA.take(KC, 2)
        gates = AFA.take(NB, 32)
        lf = AFA.take(NB, 16)
        garr = AFA.take(NB, 72)
        trif = AFA.take(128)
        trib = AFA.take(128)
        onesf = AFA.take(128)
        S = AFA.take(2, 4, 129)
        sumB = AFA.take(8)
        tmpg = AFA.take(16)
        P.dma("pool", ident_b, identf_d[:, :], writes=["ident_b"])
        P.dma("sp", convw, convw_d[:, :, :], writes=["convw"])
        P.dma("sp", halo, halo_d[:, :, :], writes=["halo"])
        P.dma("sp", gates, gates_d[:, :, :], writes=["gates"])
        P.dma("sp", trif, trif_d[:, :], writes=["trif"])
        P.dma("sp", trib, trib_d[:, :], writes=["trib"])
        P.dma("sp", onesf, onesf_d[:, :], writes=["onesf"])
        P.add("pool", lambda e: e.memset(extc[:, 0:1], 0.0), writes=["extc"])
        P.add("pool", lambda e: e.memset(extc[:, CTX + 1:CTX + 2], 0.0), writes=["extc"])
        for kc in range(KC):
            for part in range(2):
                if part == 0:
                    eb_ = ext[0]
                    ek = ("ext", 0)
                    n = TL
                    P.dma("sp", eb_[:, 1:TL + 1], pqk_d[:, kc, 0:TL], writes=[ek])
                    P.add("pool", lambda e, eb_=eb_, kc=kc: e.tensor_copy(eb_[:, 0:1], halo[:, kc, 0:1]), reads=["halo"], writes=[ek])
                    P.add("pool", lambda e, eb_=eb_, kc=kc: e.tensor_copy(eb_[:, TL + 1:TL + 2], halo[:, kc, 1:2]), reads=["halo"], writes=[ek])
                    cbase = 0
                else:
                    eb_ = extc
                    ek = "extc"
                    n = CTX
                    P.dma("sp", eb_[:, 1:CTX + 1], pqk_d[:, kc, TL:NCOL], writes=[ek])
                    cbase = TL
                for (c0, w) in col_tiles(0, n):
                    P.add("dve", lambda e, eb_=eb_, c0=c0, w=w, kc=kc: e.tensor_scalar(
                        out=cv[:, 0:w], in0=eb_[:, c0:c0 + w], scalar1=convw[:, kc, 0:1], scalar2=None, op0=ALU.mult),
                        reads=[ek, "convw"], writes=["cv"])
                    for j in (1, 2):
                        P.add("dve", lambda e, eb_=eb_, c0=c0, w=w, kc=kc, j=j: e.scalar_tensor_tensor(
                            out=cv[:, 0:w], in0=eb_[:, c0 + j:c0 + j + w], scalar=convw[:, kc, j:j + 1], in1=cv[:, 0:w],
                            op0=ALU.mult, op1=ALU.add), reads=[ek, "convw", "cv"], writes=["cv"])
                    dst = qk_s[:, kc, cbase + c0:cbase + c0 + w]
                    P.add("act", lambda e, w=w, dst=dst: e.activation(dst, cv[:, 0:w], AF.Silu),
                          reads=["cv"], writes=[("qk_s", kc)])
                    if kc < 4:
                        P.add("pool", lambda e, dst=dst: e.tensor_scalar(out=dst, in0=dst, scalar1=0.125, scalar2=None, op0=ALU.mult),
                              reads=[("qk_s", kc)], writes=[("qk_s", kc)])
        store("sp", qk_o[:, :, :], qk_s, [("qk_s", kc) for kc in range(KC)])
        g5 = gates.rearrange("p n (d a h) -> p n d a h", d=2, a=2)
        lf4 = lf.rearrange("p n (d h) -> p n d h", d=2)
        for d in range(2):
            P.add("act", lambda e, d=d: e.activation(lf4[:, :, d, :], g5[:, :, d, 1, :], AF.Exp, scale=-1.0),
                  reads=["gates"], writes=["lf"])
        P.add("act", lambda e: e.activation(lf, lf, AF.Ln, bias=1.0, scale=1.0), reads=["lf"], writes=["lf"])
        P.add("dve", lambda e: e.tensor_scalar(out=lf, in0=lf, scalar1=-1.0, scalar2=None, op0=ALU.mult), reads=["lf"], writes=["lf"])
        P.add("dve", lambda e: e.memset(sumB, 0.0), writes=["sumB"])
        for blk in range(NB):
            cp = pbank[0]
            P.mm(cp[:, 0:8], trif, lf[:, blk, 0:8], True, True, reads=["trif", "lf"], writes=[("pb", 0)])
            P.mm(cp[:, 8:16], trib, lf[:, blk, 8:16], True, True, reads=["trib", "lf"], writes=[("pb", 0)])
            P.mm(cp[:, 16:32], onesf, lf[:, blk, :], True, True, reads=["onesf", "lf"], writes=[("pb", 0)])
            lfe = lf[:, blk, :].rearrange("p (d r t) -> p d r t", d=2, t=2)
            P.mm(cp[0:64, 32:40], onesf[:, 0:64], lfe[:, :, :, 0], True, True, reads=["onesf", "lf"], writes=[("pb", 0)])
            P.mm(cp[64:128, 32:40], onesf[:, 0:64], lfe[:, :, :, 1], True, True, reads=["onesf", "lf"], writes=[("pb", 0)])
            li_v = g5[:, blk, :, 0, :]
            P.add("dve", lambda e, cp=cp, blk=blk, li_v=li_v: e.tensor_tensor(
                out=garr[:, blk, 0:16].rearrange("p (d h) -> p d h", d=2), in0=li_v,
                in1=cp[:, 0:16].rearrange("p (d h) -> p d h", d=2), op=ALU.subtract),
                reads=["gates", ("pb", 0)], writes=["garr"])
            P.add("dve", lambda e, cp=cp, blk=blk: e.tensor_tensor(out=tmpg, in0=garr[:, blk, 0:16], in1=cp[:, 16:32], op=ALU.add),
                  reads=["garr", ("pb", 0)], writes=["tmpg"])
            P.add("act", lambda e, blk=blk: e.activation(garr[:, blk, 16:32], tmpg, AF.Exp), reads=["tmpg"], writes=["garr"])
            P.add("act", lambda e, cp=cp, blk=blk: e.activation(garr[:, blk, 32:48], cp[:, 0:16], AF.Exp),
                  reads=[("pb", 0)], writes=["garr"])
            P.add("act", lambda e, cp=cp, blk=blk: e.activation(garr[:, blk, 48:56], cp[:, 32:40], AF.Exp),
                  reads=[("pb", 0)], writes=["garr"])
            if blk < NBL:
                P.add("dve", lambda e, cp=cp: e.tensor_tensor(out=sumB, in0=sumB, in1=cp[:, 32:40], op=ALU.add),
                      reads=["sumB", ("pb", 0)], writes=["sumB"])
            P.add("act", lambda e, cp=cp, blk=blk: e.copy(garr[:, blk, 56:72], cp[:, 0:16]), reads=[("pb", 0)], writes=["garr"])
        store("sp", garr_o[:, :, :], garr, ["garr"])
        P.add("act", lambda e: e.activation(sumB, sumB, AF.Exp), reads=["sumB"], writes=["sumB"])
        store("sp", Aagg_o.rearrange("p d r -> p (d r)"), sumB, ["sumB"])
        P.add("dve", lambda e: e.memset(S, 0.0), writes=[("S", 0), ("S", 1)])
        cb = list(range(NBL, NB))
        state_scan(S, [cb, cb[::-1]], ("sbuf", qk_s), va_d, garr, ident_b, None, "c")
        store("sp", Lctx_o[:, :, :, :], S, [("S", 0), ("S", 1)])
        P.fence()
        P.add("dve", lambda e: e.memset(S, 0.0), writes=[("S", 0), ("S", 1)])
        lb_ = list(range(NBL))
        state_scan(S, [lb_, lb_[::-1]], ("sbuf", qk_s), va_d, garr, ident_b, None, "l")
        store("sp", Lagg_o[:, :, :, :], S, [("S", 0), ("S", 1)])

    def stage_d():
        qk_d = din("qkT", [128, KC, NCOL], BF16)
        va_d = din("vaug1", [128, NB, 8, 129], BF16)
        sig_d = din("sigoT", [128, KC, NCOL], BF16)
        garr_d = din("garr", [128, NB, 72])
        seqA_d = din("seqA", [128, 2, 7, 4])
        seqL_d = din("seqL", [128, 2, 7, 4 * 129])
        Lctx_d = din("Lctx", [128, 2, 4, 129])
        identf_d = din("ident_f", [128, 128])
        mask_d = din("maskneg", [128, 2 * 128])
        wo1_d = din("wo1", [128, 8 * 1024])
        outg_d = din("outg", [128, 8])
        fng_d = din("fng", [128, 8])
        out_o = dout("outT", [128, KC, TL])
        ABF.reset()
        AFA.reset()
        ident_b = ABF.take(128)
        Sin = [[ABF.take(4, 129) for _ in range(NBL)] for _ in range(2)]
        hgT = ABF.take(8, TL)
        garr = AFA.take(NB, 72)
        identf = AFA.take(128)
        maskn = AFA.take(2, 128)
        S = AFA.take(2, 4, 129)
        seqA = AFA.take(2, 7, 4)
        seqLb = [AFA.take(4, 129) for _ in range(2)]
        outg = AFA.take(8)
        fng = AFA.take(8)
        P.dma("pool", ident_b, identf_d[:, :], writes=["ident_b"])
        P.dma("sp", garr, garr_d[:, :, :], writes=["garr"])
        P.dma("sp", identf, identf_d[:, :], writes=["identf"])
        P.dma("sp", maskn, mask_d.rearrange("p (d j) -> p d j", d=2), writes=["maskn"])
        P.dma("sp", seqA, seqA_d[:, :, :, :], writes=["seqA"])
        P.dma("sp", outg, outg_d[:, :], writes=["outg"])
        P.dma("sp", fng, fng_d[:, :], writes=["fng"])
        P.dma("sp", S, Lctx_d[:, :, :, :], writes=[("S", 0), ("S", 1)])
        k = 0
        for d in range(2):
            for j in range(7):
                sl = k % 2
                k += 1
                P.dma("sp", seqLb[sl], seqL_d[:, d, j].rearrange("p (r v) -> p r v", r=4), writes=[("seqLb", sl)])
                P.add("dve", lambda e, d=d, j=j: e.tensor_tensor(
                    out=S[:, d], in0=S[:, d], in1=seqA[:, d, j, :].unsqueeze(2).to_broadcast([128, 4, 129]), op=ALU.mult),
                    reads=[("S", d), "seqA"], writes=[("S", d)])
                P.add("dve", lambda e, d=d, sl=sl: e.tensor_tensor(out=S[:, d], in0=S[:, d], in1=seqLb[sl], op=ALU.add),
                      reads=[("S", d), ("seqLb", sl)], writes=[("S", d)])
        lb_ = list(range(NBL))
        state_scan(S, [lb_, lb_[::-1]], ("dram", qk_d), va_d, garr, ident_b, Sin, "l")
        P.fence()
        qc_b = [ABF.take(8, 128) for _ in range(2)]
        va_b = [ABF.take(8, 129) for _ in range(2)]
        sg_b = [ABF.take(8, 128) for _ in range(2)]
        SD = [ABF.take(128) for _ in range(2)]
        hn = ABF.take(8, 128)
        DT = [AFA.take(128) for _ in range(2)]
        bc = [AFA.take(128) for _ in range(2)]
        T1 = [AFA.take(129) for _ in range(2)]
        NUM = [AFA.take(129) for _ in range(2)]
        dn = [AFA.take(1) for _ in range(2)]
        Hacc = AFA.take(8, 128)
        sqH = AFA.take(8, 128)
        ssq = AFA.take(8)
        it = 0
        for blk in range(NBL):
            sl = blk % 2
            cs_ = slice(blk * 128, (blk + 1) * 128)
            P.dma("sp", qc_b[sl], qk_d[:, :, cs_], writes=[("qc", sl)])
            P.dma("sp", va_b[sl], va_d[:, blk], writes=[("vab", sl)])
            P.dma("sp", sg_b[sl], sig_d[:, :, cs_], writes=[("sgb", sl)])
            for d in range(2):
                for h in range(8):
                    c = d * 8 + h
                    pr, off = h // 2, 64 * (h % 2)
                    r = it % 2
                    it += 1
                    gpk = ("pb", r)
                    G = pbank[r]
                    P.add("pool", lambda e, r=r, blk=blk, c=c: e.tensor_copy(
                        bc[r], garr[:, blk, 56 + c:57 + c].to_broadcast([128, 128])), reads=["garr"], writes=[("bc", r)])
                    P.mm(G[:, 0:128], bc[r], identf, True, False, reads=[("bc", r), "identf"], writes=[gpk])
                    P.mm(G[:, 0:128], identf, maskn[:, d, :], False, True, reads=["identf", "maskn"], writes=[gpk])
                    P.add("act", lambda e, G=G, r=r, blk=blk, c=c: e.activation(
                        DT[r], G[:, 0:128], AF.Exp, bias=garr[:, blk, c:c + 1], scale=1.0),
                        reads=[gpk, "garr"], writes=[("DT", r)])
                    STp = pbank[2 + r]
                    P.mm(STp[:, 0:128], qc_b[sl][off:off + 64, 4 + pr, :], qc_b[sl][off:off + 64, pr, :], True, True,
                         reads=[("qc", sl)], writes=[("pb", 2 + r)])
                    P.add("dve", lambda e, STp=STp, r=r: e.tensor_tensor(out=SD[r], in0=STp[:, 0:128], in1=DT[r], op=ALU.mult),
                          reads=[("pb", 2 + r), ("DT", r)], writes=[("SD", r)])
                    O1 = pbank[4]
                    P.mm(O1[:, 0:129], SD[r], va_b[sl][:, h, :], True, True, reads=[("SD", r), ("vab", sl)], writes=[("pb", 4)])
                    O2 = pbank[5]
                    P.mm(O2[:, 0:129], qc_b[sl][off:off + 64, pr, :], Sin[d][blk][off:off + 64, pr, :], True, True,
                         reads=[("qc", sl), ("Sin", d, blk)], writes=[("pb", 5)])
                    P.add("act", lambda e, O2=O2, r=r, blk=blk, c=c: e.activation(
                        T1[r], O2[:, 0:129], AF.Copy, scale=garr[:, blk, 32 + c:33 + c]),
                        reads=[("pb", 5), "garr"], writes=[("T1", r)])
                    P.add("dve", lambda e, O1=O1, r=r: e.tensor_tensor(out=NUM[r], in0=T1[r], in1=O1[:, 0:129], op=ALU.add),
                          reads=[("T1", r), ("pb", 4)], writes=[("NUM", r)])
                    P.add("act", lambda e, r=r: e.activation(dn[r], NUM[r][:, 128:129], AF.Abs),
                          reads=[("NUM", r)], writes=[("dn", r)])
                    P.add("dve", lambda e, r=r: e.tensor_scalar(out=dn[r], in0=dn[r], scalar1=1.0, scalar2=None, op0=ALU.max),
                          reads=[("dn", r)], writes=[("dn", r)])
                    P.add("dve", lambda e, r=r: e.reciprocal(dn[r], dn[r]), reads=[("dn", r)], writes=[("dn", r)])
                    if d == 0:
                        P.add("dve", lambda e, r=r, h=h: e.tensor_scalar(
                            out=Hacc[:, h, :], in0=NUM[r][:, 0:128], scalar1=dn[r][:, 0:1], scalar2=None, op0=ALU.mult),
                            reads=[("NUM", r), ("dn", r)], writes=[("Hacc", h)])
                    else:
                        P.add("dve", lambda e, r=r, h=h: e.scalar_tensor_tensor(
                            out=Hacc[:, h, :], in0=NUM[r][:, 0:128], scalar=dn[r][:, 0:1], in1=Hacc[:, h, :],
                            op0=ALU.mult, op1=ALU.add), reads=[("NUM", r), ("dn", r), ("Hacc", h)], writes=[("Hacc", h)])
            hk = [("Hacc", h) for h in range(8)]
            P.add("pool", lambda e: e.tensor_tensor(out=sqH, in0=Hacc, in1=Hacc, op=ALU.mult), reads=hk, writes=["sqH"])
            P.add("dve", lambda e: e.tensor_reduce(out=ssq, in_=sqH, axis=AX.X, op=ALU.add), reads=["sqH"], writes=["ssq"])
            P.add("act", lambda e: e.activation(ssq, ssq, AF.Sqrt, bias=EPS, scale=1.0 / 128), reads=["ssq"], writes=["ssq"])
            P.add("dve", lambda e: e.reciprocal(ssq, ssq), reads=["ssq"], writes=["ssq"])
            P.add("dve", lambda e: e.tensor_tensor(out=hn, in0=Hacc, in1=ssq.unsqueeze(2).to_broadcast([128, 8, 128]), op=ALU.mult),
                  reads=hk + ["ssq"], writes=["hn"])
            for h in range(8):
                P.add("pe", lambda e, h=h: e.transpose(pbf[:, h * 128:(h + 1) * 128], hn[:, h, :], ident_b),
                      reads=["hn", "ident_b"], writes=["pbf"])
            P.add("dve", lambda e, sl=sl: e.tensor_tensor(
                out=sg_b[sl], in0=sg_b[sl], in1=outg.unsqueeze(2).to_broadcast([128, 8, 128]), op=ALU.mult),
                reads=[("sgb", sl), "outg"], writes=[("sgb", sl)])
            P.add("dve", lambda e, sl=sl, cs_=cs_: e.tensor_tensor(
                out=hgT[:, :, cs_], in0=pbf[:, :].rearrange("p (h t) -> p h t", h=8), in1=sg_b[sl], op=ALU.mult),
                reads=["pbf", ("sgb", sl)], writes=[("hgT", blk // 4)])
        P.fence()
        wo1 = ABF.take(8, 1024)
        P.dma("pool", wo1, wo1_d.rearrange("p (c f) -> p c f", c=8), writes=["wo1"])
        for ti, (c0, w, s) in enumerate(lat_tiles):
            for oc in range(KC):
                b = oc % 2
                yp = pbank[b]
                for h in range(8):
                    P.mm(yp[:, 0:w], wo1[:, h, oc * 128:(oc + 1) * 128], hgT[:, h, c0:c0 + w], h == 0, h == 7,
                         reads=["wo1", ("hgT", ti)], writes=[("pb", b)])
                P.add("dve", lambda e, yp=yp, w=w, oc=oc, c0=c0: e.scalar_tensor_tensor(
                    out=hT[:, oc, c0:c0 + w], in0=yp[:, 0:w], scalar=Gvec[:, 1, 1, oc, 0:1],
                    in1=hT[:, oc, c0:c0 + w], op0=ALU.mult, op1=ALU.add),
                    reads=[("pb", b), "Gvec", ("hT", oc, c0)], writes=[("hT", oc, c0)])
        ffn(1, 1, with_ctx=False)
        P.fence()
        ABF.reset()
        AFA.reset()
        sq = ABF.take(KC, 512)
        rstd = AFA.take(512)
        ob = [AFA.take(512) for _ in range(2)]
        fng2 = AFA.take(8)
        P.dma("sp", fng2, fng_d[:, :], writes=["fng2"])
        k = 0
        for (c0, w, s) in lat_tiles:
            ssb = pbank[6]
            for kc in range(KC):
                P.add("act", lambda e, kc=kc, c0=c0, w=w: e.activation(sq[:, kc, 0:w], hT[:, kc, c0:c0 + w], AF.Square),
                      reads=[("hT", kc, c0)], writes=[("sq", kc)])
            for kc in range(KC):
                P.mm(ssb[:, 0:w], onesb[:], sq[:, kc, 0:w], kc == 0, kc == KC - 1, reads=["onesb", ("sq", kc)], writes=[("pb", 6)])
            P.add("act", lambda e, w=w, ssb=ssb: e.activation(rstd[:, 0:w], ssb[:, 0:w], AF.Sqrt, bias=EPS, scale=1.0 / D),
                  reads=[("pb", 6)], writes=["rstd"])
            P.add("dve", lambda e, w=w: e.reciprocal(rstd[:, 0:w], rstd[:, 0:w]), reads=["rstd"], writes=["rstd"])
            for kc in range(KC):
                sl = k % 2
                k += 1
                P.add("dve", lambda e, kc=kc, c0=c0, w=w, sl=sl: e.scalar_tensor_tensor(
                    out=ob[sl][:, 0:w], in0=hT[:, kc, c0:c0 + w], scalar=fng2[:, kc:kc + 1], in1=rstd[:, 0:w],
                    op0=ALU.mult, op1=ALU.mult), reads=[("hT", kc, c0), "fng2", "rstd"], writes=[("ob", sl)])
                store("sp", out_o[:, kc, c0:c0 + w], ob[sl][:, 0:w], [("ob", sl)])

    if stage == "M":
        compute_mod()
        modT_out = dout("modT_out", [128, 2 * 72 * 2])
        store("sp", modT_out[:, :], modT[:].rearrange("p l c s -> p (l c s)"), ["modT"])
    elif stage == "A":
        xT = din("xT", [128, KC, NCOL])
        modT_in = din("modT_in", [128, 2 * 72 * 2])
        P.dma("sp", hT[:], xT[:, :, :], writes=hkeys)
        P.dma("sp", modT[:].rearrange("p l c s -> p (l c s)"), modT_in[:, :], writes=["modT"])
        derive_mod()
        ffn(0, 0)
        hT_out = dout("hT_out", [128, KC, NCOL])
        mixer0_project()
        store("sp", hT_out[:, :, :], hT[:], hkeys)
    elif stage == "B":
        hT_in = din("hT_in", [128, KC, NCOL])
        modT_in = din("modT_in", [128, 2 * 72 * 2])
        P.dma("sp", hT[:], hT_in[:, :, :], writes=hkeys)
        P.dma("sp", modT[:].rearrange("p l c s -> p (l c s)"), modT_in[:, :], writes=["modT"])
        derive_mod()
        mixer0_attend()
        ffn(0, 1)
        ffn(1, 0)
        mixer1_project()
        hT_out = dout("hT_out", [128, KC, NCOL])
        store("sp", hT_out[:, :, :], hT[:], hkeys)
    elif stage == "C":
        stage_c()
    elif stage == "D":
        hT_in = din("hT_in", [128, KC, NCOL])
        modT_in = din("modT_in", [128, 2 * 72 * 2])
        P.dma("sp", hT[:], hT_in[:, :, :], writes=hkeys)
        P.dma("sp", modT[:].rearrange("p l c s -> p (l c s)"), modT_in[:, :], writes=["modT"])
        derive_mod()
        stage_d()
    finish()
    return nc, es


def fm(a):
    T = a.shape[0]
    return np.ascontiguousarray(a.reshape(T, KC, 128).transpose(2, 1, 0))


def vec_fm(v, nch):
    return np.ascontiguousarray(v.reshape(nch, 128).T)


def wfm(w):
    N = w.shape[1]
    return np.ascontiguousarray(w.reshape(KC, 128, N).transpose(1, 0, 2).reshape(128, KC * N))


def rope_tables(t0, TL):
    GRID_W, HD, THETA = 64, 64, 10000.0
    axis_dim = HD // 2
    inv = (THETA ** (-np.arange(0, axis_dim, 2, dtype=np.float32) / axis_dim)).astype(np.float32)
    t = np.arange(t0, t0 + TL)
    row = (t // GRID_W).astype(np.float32)
    col = (t % GRID_W).astype(np.float32)
    cos = np.zeros((128, TL), np.float32)
    sin = np.zeros((128, TL), np.float32)
    for p in range(128):
        d = p % 64
        pos = row if d < 32 else col
        ang = (pos * inv[d % 16]).astype(np.float32)
        cos[p] = np.cos(ang)
        sin[p] = np.sin(ang)
    return cos, sin


def const_mats():
    blk = np.zeros((128, 128), np.float32)
    blk[:64, :64] = 1
    blk[64:, 64:] = 1
    rr = np.zeros((128, 128), np.float32)
    for m in range(128):
        if m % 32 < 16:
            rr[m + 16, m] = -1
        else:
            rr[m - 16, m] = 1
    return blk, rr


def prep_common(inp):
    d = {}
    c = np.asarray(inp["c"], np.float32).reshape(D)
    cc = np.asarray(inp["c_ctx"], np.float32).reshape(D)
    d["cvec"] = np.ascontiguousarray(np.stack([vec_fm(c, KC), vec_fm(cc, KC)], axis=-1))
    mw = np.asarray(inp["mod_w"], np.float32)
    d["modw"] = np.ascontiguousarray(
        mw.reshape(2, KC, 128, 36, 256).transpose(0, 3, 2, 1, 4).reshape(2, 36, 128, KC * 256))
    mb = np.asarray(inp["mod_b"], np.float32)
    d["modb"] = np.ascontiguousarray(mb.reshape(2, 72, 128).transpose(0, 2, 1))
    ng = np.asarray(inp["norm_g"], np.float32)
    d["normg"] = np.ascontiguousarray(ng.reshape(2, 3, KC, 128).transpose(0, 3, 1, 2))
    w13 = np.asarray(inp["ffn_w13"], np.float32).reshape(4, KC, 128, 2, NJ, 128)
    d["w13"] = np.ascontiguousarray(w13.transpose(0, 4, 2, 1, 3, 5).reshape(4, NJ, 128, KC * 2 * 128))
    w2 = np.asarray(inp["ffn_w2"], np.float32).reshape(4, NJ, 128, KC, 128)
    d["w2"] = np.ascontiguousarray(w2.transpose(0, 3, 2, 1, 4).reshape(4, KC, 128, NJ * 128))
    d["onesb"] = np.ones((128, 128), np.float32)
    wi = np.asarray(inp["ab_w_in"], np.float32)[0]
    u, v, q, k, va = wi[:, 0:512], wi[:, 512:1024], wi[:, 1024:1536], wi[:, 1536:1664], wi[:, 1664:1792]
    kd = np.concatenate([k[:, 0:64], k[:, 0:64], k[:, 64:128], k[:, 64:128]], axis=1)
    d["winfm"] = wfm(np.concatenate([u, q, kd], axis=1))
    d["wintm"] = wfm(np.concatenate([v, va], axis=1))
    spw = np.asarray(inp["ab_spatial_w"], np.float32)[0]
    d["wsT"] = np.ascontiguousarray(spw.transpose(2, 0, 1).reshape(128, 8 * 128))
    d["bsrow"] = np.ascontiguousarray(np.asarray(inp["ab_spatial_b"], np.float32)[0].reshape(1, 8 * 128))
    d["gng_rep"] = np.ascontiguousarray(np.broadcast_to(np.asarray(inp["ab_gate_norm_g"], np.float32)[0][None, :], (128, 512)))
    gq = np.asarray(inp["ab_q_norm_g"], np.float32)[0]
    gk = np.asarray(inp["ab_k_norm_g"], np.float32)[0]
    d["gqk_pp"] = np.ascontiguousarray(np.stack([np.tile(gq, 2), np.tile(gk, 2)], axis=1))
    d["gqk_rep"] = np.ascontiguousarray(np.broadcast_to(np.stack([gq, gk])[None], (128, 2, 64)))
    blk, rr = const_mats()
    d["blk64"], d["rrot"] = blk, rr
    wo = np.asarray(inp["ab_w_out"], np.float32)[0]
    d["wo_g"] = np.ascontiguousarray(wo[0:512].reshape(4, 128, 1024).transpose(1, 0, 2).reshape(128, 4 * 1024))
    d["wo_a"] = np.ascontiguousarray(wo[512:].reshape(8, 64, 1024).transpose(1, 0, 2).reshape(64, 8 * 1024))
    return d


_PROGS = {}


def get_prog(TL, stage, NK):
    key = (TL, stage, NK)
    if key not in _PROGS:
        _PROGS[key] = build_program(TL, stage, NK)
    return _PROGS[key][0]


def pick(d, names):
    return {k: d[k] for k in names}


def prep_layer1(inp, d):
    wi = np.asarray(inp["ml_w_in"], np.float32)[0]
    q, k, v, o, g = wi[:, 0:512], wi[:, 512:1024], wi[:, 1024:2048], wi[:, 2048:3072], wi[:, 3072:3104]
    d["w1fm"] = wfm(np.concatenate([q, k, o], axis=1))
    d["w1tm"] = wfm(np.concatenate([v, g], axis=1))
    d["gateb_rep"] = np.ascontiguousarray(np.broadcast_to(np.asarray(inp["ml_gate_b"], np.float32)[0][None, :], (128, 32)))
    cw = np.asarray(inp["ml_conv_w"], np.float32)[0]
    d["convw"] = np.ascontiguousarray(cw.reshape(3, KC, 128).transpose(2, 1, 0))
    one = np.ones((128, 128), np.float32)
    d["tri_f"] = np.triu(one)
    d["tri_b"] = np.tril(one)
    d["ones_f"] = one
    d["ident_f"] = np.eye(128, dtype=np.float32)
    s = np.arange(128)[:, None]
    j = np.arange(128)[None, :]
    mf = np.where(s > j, -30000.0, 0.0).astype(np.float32)
    mb = np.where(s < j, -30000.0, 0.0).astype(np.float32)
    d["maskneg"] = np.ascontiguousarray(np.stack([mf, mb], axis=1).reshape(128, 256))
    wo = np.asarray(inp["ml_w_out"], np.float32)[0]
    d["wo1"] = np.ascontiguousarray(wo.reshape(8, 128, 1024).transpose(1, 0, 2).reshape(128, 8 * 1024))
    d["outg"] = vec_fm(np.asarray(inp["ml_out_norm_g"], np.float32)[0], 8)
    d["fng"] = vec_fm(np.asarray(inp["final_norm_g"], np.float32), 8)


M_IN = ["cvec", "modw", "modb", "normg", "onesb"]
A_IN = ["normg", "onesb", "winfm", "wintm", "wsT", "bsrow", "gng_rep", "gqk_pp", "blk64", "rrot"]
B_IN = ["normg", "onesb", "gqk_rep", "wo_g", "wo_a", "w1fm", "w1tm", "gateb_rep"]
C_IN = ["normg", "onesb", "convw", "tri_f", "tri_b", "ones_f", "ident_f"]
D_IN = ["normg", "onesb", "ident_f", "maskneg", "wo1", "outg", "fng"]


def _run(nc, maps, ids, tag):
    import os, time as _t
    if os.environ.get("KTRACE"):
        t0 = _t.time()
        r = run_bass_kernel_spmd(nc, maps, core_ids=ids, trace=True)
        print("STAGE", tag, "exec_time_ns", r.exec_time_ns, "wall", _t.time() - t0, flush=True)
        return r.results
    return run_bass_kernel_spmd(nc, maps, core_ids=ids).results


def run_all(inputs, TL, ncores, debug=None):
    x = np.asarray(inputs["x"], np.float32)[0]
    ctx = np.asarray(inputs["ctx"], np.float32)[0]
    NK = TL * ncores
    ids = list(range(ncores))
    com = prep_common(inputs)
    prep_layer1(inputs, com)
    ctxT = fm(ctx)
    rM = _run(get_prog(TL, "M", NK), [pick(com, M_IN)], [0], "M")
    modT_all = rM[0]["modT_out"]
    w13a, w2a = com["w13"], com["w2"]
    maps = []
    for i in ids:
        m = pick(com, A_IN)
        m.update(modT_in=modT_all, w13=w13a[0:1], w2=w2a[0:1])
        m["xT"] = np.concatenate([fm(x[i * TL:(i + 1) * TL]), ctxT], axis=2)
        m["cosT"], m["sinT"] = rope_tables(i * TL, TL)
        maps.append(m)
    rA = _run(get_prog(TL, "A", NK), maps, ids, "A")
    kT_all = np.ascontiguousarray(np.concatenate([r["kT"][:, :, 0:TL] for r in rA] + [rA[0]["kT"][:, :, TL:]], axis=2))
    va_all = np.ascontiguousarray(np.concatenate([r["vaug"][:, 0:TL // 128] for r in rA] + [rA[0]["vaug"][:, TL // 128:]], axis=1))
    maps = []
    for i in ids:
        m = pick(com, B_IN)
        m.update(hT_in=rA[i]["hT_out"], modT_in=modT_all, qT=rA[i]["qT"], gT=rA[i]["gT"],
                 kT_all=kT_all, vaug_all=va_all, w13=w13a[1:3], w2=w2a[1:3])
        maps.append(m)
    rB = _run(get_prog(TL, "B", NK), maps, ids, "B")
    maps = []
    for i in ids:
        m = pick(com, C_IN)
        halo = np.zeros((128, KC, 2), np.float32)
        if i > 0:
            halo[:, :, 0] = rB[i - 1]["pqkT"][:, :, TL - 1]
        if i < ncores - 1:
            halo[:, :, 1] = rB[i + 1]["pqkT"][:, :, 0]
        m.update(pqkT=rB[i]["pqkT"], halo=halo, gates=rB[i]["gates"], vaug1=rB[i]["vaug1"])
        maps.append(m)
    rC = _run(get_prog(TL, "C", NK), maps, ids, "C")
    maps = []
    for i in ids:
        m = pick(com, D_IN)
        seqA = np.ones((128, 2, 7, 4), np.float32)
        seqL = np.zeros((128, 2, 7, 4 * 129), np.float32)
        for slot, j in enumerate(range(0, i)):
            seqA[:, 0, slot] = rC[j]["Aagg"][:, 0]
            seqL[:, 0, slot] = rC[j]["Lagg"][:, 0].reshape(128, 516)
        for slot, j in enumerate(range(ncores - 1, i, -1)):
            seqA[:, 1, slot] = rC[j]["Aagg"][:, 1]
            seqL[:, 1, slot] = rC[j]["Lagg"][:, 1].reshape(128, 516)
        m.update(hT_in=rB[i]["hT_out"], modT_in=modT_all, qkT=rC[i]["qkT"], vaug1=rB[i]["vaug1"],
                 sigoT=rB[i]["sigoT"], garr=rC[i]["garr"], seqA=seqA, seqL=seqL, Lctx=rC[0]["Lctx"],
                 w13=w13a[3:4], w2=w2a[3:4])
        maps.append(m)
    rD = _run(get_prog(TL, "D", NK), maps, ids, "D")
    if debug is not None:
        debug.update(rA=rA, rB=rB, rC=rC, rD=rD)
    out = np.concatenate([r["outT"].transpose(2, 1, 0).reshape(TL, D) for r in rD], axis=0)
    return out[None].astype(np.float32)


def kernel(**inputs):
    return run_all(inputs, SEQ // N_CORES, N_CORES)
```
